# Optimizing a Trainium2 kernel written in Bass

```python
import jax, jax.numpy as jnp
from jax import lax
import numpy as np

D_MODEL = 1024
BATCH = 8
SEQ = 4096
DEPTH = 4

GDN_HEADS = 4
GDN_DK = 128
GDN_DV = 128
GDN_CHUNK = 64
FOX_HEADS = 8
FOX_DH = 64
FOX_BLOCK = 128
MLSTM_HEADS = 4
MLSTM_DK = 64
MLSTM_DV = 128
MLSTM_CHUNK = 64
CONV_WIDTH = 4
N_BRANCH = 3
NORM_EPS = 1e-6

GDN_W = GDN_HEADS * GDN_DV
FOX_W = FOX_HEADS * FOX_DH
MLSTM_W = MLSTM_HEADS * MLSTM_DV
GDN_QKV = 2 * GDN_HEADS * GDN_DK + GDN_W
MLSTM_QK = 2 * MLSTM_HEADS * MLSTM_DK

IN_SPLITS = (
    ('a_qkv', GDN_QKV), ('a_beta', GDN_HEADS), ('a_alpha', GDN_HEADS), ('a_z', GDN_W),
    ('b_qkv', 3 * FOX_W), ('b_f', FOX_HEADS), ('b_z', FOX_W),
    ('c_qk', MLSTM_QK), ('c_v', MLSTM_W), ('c_i', MLSTM_HEADS), ('c_f', MLSTM_HEADS),
    ('c_o', MLSTM_W), ('c_z', MLSTM_W),
    ('gate', N_BRANCH * D_MODEL),
)
D_IN = sum(w for _, w in IN_SPLITS)

kernel_name = 'hybrid_gdn_fox_mlstm_trunk'


def split_cols(u):
    out, off = {}, 0
    for name, w in IN_SPLITS:
        out[name] = u[..., off:off + w]
        off += w
    return out


def rms_norm(x, g):
    xf = x.astype(jnp.float32)
    y = xf * lax.rsqrt(jnp.mean(xf * xf, axis=-1, keepdims=True) + NORM_EPS)
    return (y * g.astype(jnp.float32)).astype(x.dtype)


def l2_normalize(x):
    return x * lax.rsqrt(jnp.sum(x * x, axis=-1, keepdims=True) + NORM_EPS)


def causal_conv_silu(u, w):
    y = lax.conv_general_dilated(
        u, w[:, None, :].astype(u.dtype), window_strides=(1,),
        padding=[(CONV_WIDTH - 1, 0)], dimension_numbers=('NWC', 'WIO', 'NWC'),
        feature_group_count=u.shape[-1])
    return jax.nn.silu(y)


def to_chunks(a, size):
    b, t, h = a.shape[:3]
    return a.reshape(b, t // size, size, h, *a.shape[3:]).swapaxes(2, 3)


def from_chunks(a):
    b, n, h, l, d = a.shape
    return a.swapaxes(2, 3).reshape(b, n * l, h, d)


def gated_delta_rule(q, k, v, beta, log_decay):
    f32 = jnp.float32
    bsz, t, h, dk = q.shape
    dv = v.shape[-1]
    L = GDN_CHUNK
    q = l2_normalize(q.astype(f32)) * dk ** -0.5
    k = l2_normalize(k.astype(f32))
    qc, kc, vc = to_chunks(q, L), to_chunks(k, L), to_chunks(v.astype(f32), L)
    bc, gc = to_chunks(beta.astype(f32), L), to_chunks(log_decay.astype(f32), L)
    gcum = jnp.cumsum(gc, axis=-1)
    causal = jnp.tril(jnp.ones((L, L), bool))
    strict = jnp.tril(jnp.ones((L, L), bool), -1)
    decay = jnp.exp(jnp.where(causal, gcum[..., :, None] - gcum[..., None, :], -jnp.inf))
    kb = kc * bc[..., None]
    lower = jnp.where(strict, jnp.einsum('bnhid,bnhjd->bnhij', kb, kc) * decay, 0.0)
    a_mat = lower + jnp.eye(L, dtype=f32)
    u = lax.linalg.triangular_solve(a_mat, vc * bc[..., None], left_side=True, lower=True, unit_diagonal=True)
    w = lax.linalg.triangular_solve(a_mat, kb * jnp.exp(gcum)[..., None], left_side=True, lower=True, unit_diagonal=True)
    qk = jnp.where(causal, jnp.einsum('bnhid,bnhjd->bnhij', qc, kc) * decay, 0.0)
    q_dec = qc * jnp.exp(gcum)[..., None]
    k_dec = kc * jnp.exp(gcum[..., -1:] - gcum)[..., None]
    g_last = jnp.exp(gcum[..., -1])

    def step(S, xs):
        u_i, w_i, qk_i, qd_i, kd_i, gl_i = xs
        v_new = u_i - jnp.einsum('bhlk,bhkv->bhlv', w_i, S)
        o = jnp.einsum('bhlk,bhkv->bhlv', qd_i, S) + jnp.einsum('bhij,bhjv->bhiv', qk_i, v_new)
        S = S * gl_i[..., None, None] + jnp.einsum('bhlk,bhlv->bhkv', kd_i, v_new)
        return S, o

    xs = tuple(jnp.moveaxis(a, 1, 0) for a in (u, w, qk, q_dec, k_dec, g_last))
    S0 = jnp.zeros((bsz, h, dk, dv), f32)
    _, o = lax.scan(step, S0, xs)
    return from_chunks(jnp.moveaxis(o, 0, 1))


def forgetting_attention(q, k, v, log_f):
    f32 = jnp.float32
    bsz, t, h, d = q.shape
    nb = t // FOX_BLOCK
    c = jnp.cumsum(log_f, axis=1).swapaxes(1, 2)
    kf, vf = k.astype(f32), v.astype(f32)
    qb = (q.astype(f32) * d ** -0.5).reshape(bsz, nb, FOX_BLOCK, h, d).swapaxes(0, 1)
    cb = c.reshape(bsz, h, nb, FOX_BLOCK).transpose(2, 0, 1, 3)
    key_pos = jnp.arange(t)

    def one_block(args):
        blk, q_blk, c_blk = args
        s = jnp.einsum('bqhd,bkhd->bhqk', q_blk, kf) + c_blk[..., :, None] - c[:, :, None, :]
        q_pos = blk * FOX_BLOCK + jnp.arange(FOX_BLOCK)
        s = jnp.where(key_pos[None, :] <= q_pos[:, None], s, -jnp.inf)
        p = jax.nn.softmax(s, axis=-1)
        return jnp.einsum('bhqk,bkhd->bqhd', p, vf)

    o = lax.map(one_block, (jnp.arange(nb), qb, cb))
    return o.swapaxes(0, 1).reshape(bsz, t, h, d)


def mlstm_chunkwise(q, k, v, i_pre, f_pre):
    f32 = jnp.float32
    bsz, t, h, dk = q.shape
    dv = v.shape[-1]
    L = MLSTM_CHUNK
    qc = to_chunks(q.astype(f32) * dk ** -0.5, L)
    kc, vc = to_chunks(k.astype(f32), L), to_chunks(v.astype(f32), L)
    ic = to_chunks(i_pre.astype(f32), L)
    b = jnp.cumsum(to_chunks(jax.nn.log_sigmoid(f_pre.astype(f32)), L), axis=-1)
    b_last = b[..., -1]
    a = b_last[..., None] - b + ic
    m_loc = jnp.max(a, axis=-1)
    wgt = jnp.exp(a - m_loc[..., None])
    dC = jnp.einsum('bnhl,bnhlk,bnhlv->bnhkv', wgt, kc, vc)
    dn = jnp.einsum('bnhl,bnhlk->bnhk', wgt, kc)

    def step(carry, xs):
        C, n, m = carry
        bl, ml, dC_i, dn_i = xs
        m_new = jnp.maximum(bl + m, ml)
        s_old = jnp.exp(bl + m - m_new)
        s_loc = jnp.exp(ml - m_new)
        C_new = C * s_old[..., None, None] + dC_i * s_loc[..., None, None]
        n_new = n * s_old[..., None] + dn_i * s_loc[..., None]
        return (C_new, n_new, m_new), (C, n, m)

    init = (jnp.zeros((bsz, h, dk, dv), f32), jnp.zeros((bsz, h, dk), f32), jnp.zeros((bsz, h), f32))
    xs = tuple(jnp.moveaxis(z, 1, 0) for z in (b_last, m_loc, dC, dn))
    _, (C_prev, n_prev, m_prev) = lax.scan(step, init, xs)
    C_prev, n_prev, m_prev = (jnp.moveaxis(z, 0, 1) for z in (C_prev, n_prev, m_prev))
    causal = jnp.tril(jnp.ones((L, L), bool))
    log_d = jnp.where(causal, b[..., :, None] - b[..., None, :] + ic[..., None, :], -jnp.inf)
    inter = b + m_prev[..., None]
    m_row = jnp.maximum(inter, jnp.max(log_d, axis=-1))
    s_inter = jnp.exp(inter - m_row)
    qk = jnp.einsum('bnhik,bnhjk->bnhij', qc, kc) * jnp.exp(log_d - m_row[..., None])
    num = jnp.einsum('bnhij,bnhjv->bnhiv', qk, vc) + s_inter[..., None] * jnp.einsum('bnhik,bnhkv->bnhiv', qc, C_prev)
    den = jnp.sum(qk, axis=-1) + s_inter * jnp.einsum('bnhik,bnhk->bnhi', qc, n_prev)
    hc = num / jnp.maximum(jnp.abs(den), jnp.exp(-m_row))[..., None]
    return from_chunks(hc)


def hybrid_layer(x, norm_g, w_in, conv_a, a_log, dt_bias, norm_a, fox_f_bias, conv_c,
                 mlstm_i_bias, mlstm_f_bias, norm_c, proj_a, proj_b, proj_c, w_out):
    f32 = jnp.float32
    bsz, t, _ = x.shape
    h = rms_norm(x, norm_g)
    u = split_cols(h @ w_in.astype(x.dtype))

    qkv = causal_conv_silu(u['a_qkv'], conv_a)
    qa, ka, va = jnp.split(qkv, [GDN_HEADS * GDN_DK, 2 * GDN_HEADS * GDN_DK], axis=-1)
    beta = jax.nn.sigmoid(u['a_beta'].astype(f32))
    log_decay = -jnp.exp(a_log.astype(f32)) * jax.nn.softplus(u['a_alpha'].astype(f32) + dt_bias.astype(f32))
    oa = gated_delta_rule(qa.reshape(bsz, t, GDN_HEADS, GDN_DK), ka.reshape(bsz, t, GDN_HEADS, GDN_DK),
                          va.reshape(bsz, t, GDN_HEADS, GDN_DV), beta, log_decay)
    ya = rms_norm(oa, norm_a.reshape(GDN_HEADS, GDN_DV)).reshape(bsz, t, GDN_W).astype(x.dtype) * jax.nn.silu(u['a_z'])

    qb, kb, vb = jnp.split(u['b_qkv'], 3, axis=-1)
    log_f = jax.nn.log_sigmoid(u['b_f'].astype(f32) + fox_f_bias.astype(f32))
    ob = forgetting_attention(qb.reshape(bsz, t, FOX_HEADS, FOX_DH), kb.reshape(bsz, t, FOX_HEADS, FOX_DH),
                              vb.reshape(bsz, t, FOX_HEADS, FOX_DH), log_f)
    yb = ob.reshape(bsz, t, FOX_W).astype(x.dtype) * jax.nn.silu(u['b_z'])

    qk_c = causal_conv_silu(u['c_qk'], conv_c)
    qc, kc = jnp.split(qk_c, 2, axis=-1)
    hc = mlstm_chunkwise(qc.reshape(bsz, t, MLSTM_HEADS, MLSTM_DK), kc.reshape(bsz, t, MLSTM_HEADS, MLSTM_DK),
                         u['c_v'].reshape(bsz, t, MLSTM_HEADS, MLSTM_DV),
                         u['c_i'] + mlstm_i_bias.astype(x.dtype), u['c_f'] + mlstm_f_bias.astype(x.dtype))
    hc = rms_norm(hc, norm_c.reshape(MLSTM_HEADS, MLSTM_DV)).reshape(bsz, t, MLSTM_W).astype(x.dtype)
    yc = jax.nn.sigmoid(u['c_o']) * hc * jax.nn.silu(u['c_z'])

    gates = jax.nn.sigmoid(u['gate']).reshape(bsz, t, N_BRANCH, D_MODEL)
    merged = (gates[:, :, 0] * (ya @ proj_a.astype(x.dtype))
              + gates[:, :, 1] * (yb @ proj_b.astype(x.dtype))
              + gates[:, :, 2] * (yc @ proj_c.astype(x.dtype)))
    return x + merged @ w_out.astype(x.dtype)


def setup_inputs(seed: int = 0) -> dict:
    key = jax.random.key(seed)
    ks = jax.random.split(key, 18)
    f32 = jnp.float32

    def normal(k, shape, scale):
        return jax.random.normal(k, shape, f32) * scale

    x = normal(ks[0], (BATCH, SEQ, D_MODEL), 1.0)
    norm_g = 1.0 + normal(ks[1], (DEPTH, D_MODEL), 0.02)
    w_in = normal(ks[2], (DEPTH, D_MODEL, D_IN), D_MODEL ** -0.5)
    conv_a = normal(ks[3], (DEPTH, CONV_WIDTH, GDN_QKV), CONV_WIDTH ** -0.5)
    a_log = jnp.log(jax.random.uniform(ks[4], (DEPTH, GDN_HEADS), f32, 1.0, 16.0))
    dt = jnp.exp(jax.random.uniform(ks[5], (DEPTH, GDN_HEADS), f32, float(np.log(1e-3)), float(np.log(1e-1))))
    dt_bias = dt + jnp.log(-jnp.expm1(-dt))
    norm_a = 1.0 + normal(ks[6], (DEPTH, GDN_W), 0.02)
    fox_f_bias = jax.random.uniform(ks[7], (DEPTH, FOX_HEADS), f32, 2.0, 6.0)
    conv_c = normal(ks[8], (DEPTH, CONV_WIDTH, MLSTM_QK), CONV_WIDTH ** -0.5)
    mlstm_i_bias = normal(ks[9], (DEPTH, MLSTM_HEADS), 0.1)
    mlstm_f_bias = jax.random.uniform(ks[10], (DEPTH, MLSTM_HEADS), f32, 3.0, 6.0)
    norm_c = 1.0 + normal(ks[11], (DEPTH, MLSTM_W), 0.02)
    proj_a = normal(ks[12], (DEPTH, GDN_W, D_MODEL), GDN_W ** -0.5)
    proj_b = normal(ks[13], (DEPTH, FOX_W, D_MODEL), FOX_W ** -0.5)
    proj_c = normal(ks[14], (DEPTH, MLSTM_W, D_MODEL), MLSTM_W ** -0.5)
    w_out = normal(ks[15], (DEPTH, D_MODEL, D_MODEL), D_MODEL ** -0.5)
    final_g = 1.0 + normal(ks[16], (D_MODEL,), 0.02)
    return {'x': x, 'norm_g': norm_g, 'w_in': w_in, 'conv_a': conv_a, 'a_log': a_log,
            'dt_bias': dt_bias, 'norm_a': norm_a, 'fox_f_bias': fox_f_bias, 'conv_c': conv_c,
            'mlstm_i_bias': mlstm_i_bias, 'mlstm_f_bias': mlstm_f_bias, 'norm_c': norm_c,
            'proj_a': proj_a, 'proj_b': proj_b, 'proj_c': proj_c, 'w_out': w_out, 'final_g': final_g}


def reference(x, norm_g, w_in, conv_a, a_log, dt_bias, norm_a, fox_f_bias, conv_c,
              mlstm_i_bias, mlstm_f_bias, norm_c, proj_a, proj_b, proj_c, w_out, final_g):
    for l in range(DEPTH):
        x = hybrid_layer(x, norm_g[l], w_in[l], conv_a[l], a_log[l], dt_bias[l], norm_a[l],
                         fox_f_bias[l], conv_c[l], mlstm_i_bias[l], mlstm_f_bias[l], norm_c[l],
                         proj_a[l], proj_b[l], proj_c[l], w_out[l])
    return rms_norm(x, final_g)
```

```python
import numpy as np
import concourse.bass as bass
import concourse.mybir as mybir
from concourse.bass_utils import run_bass_kernel_spmd

F32 = mybir.dt.float32
BF16 = mybir.dt.bfloat16
ALU = mybir.AluOpType
AF = mybir.ActivationFunctionType
AX = mybir.AxisListType


class Prog:
    NSQ = 6
    EPOCH = 30000
    DEPOCH = 6 * 3000

    def __init__(self, nc):
        self.nc = nc
        self.ops = []
        self.last_w = {}
        self.readers = {}
        self.ndma = {}
        self.bar = None
        self.lastop = {}
        self.dmaops = {}

    def barrier(self):
        deps = set(self.lastop.values())
        for q, lst in self.dmaops.items():
            deps.update(lst[-self.NSQ:])
        nc = self.nc
        idx = self._add("pool", lambda e: e.memset(self.bar_tile[:], 0.0), (), (), "c")
        self.ops[idx]["deps"].update(deps)
        self.bar = idx

    def _add(self, eng, fn, r, w, kind):
        idx = len(self.ops)
        deps = set()
        for t in r:
            if t in self.last_w:
                deps.add(self.last_w[t])
        for t in w:
            if t in self.last_w:
                deps.add(self.last_w[t])
            for x in self.readers.get(t, ()):
                deps.add(x)
        deps.discard(idx)
        if self.bar is not None:
            deps.add(self.bar)
        for t in w:
            self.last_w[t] = idx
            self.readers[t] = []
        for t in r:
            if t not in w:
                self.readers.setdefault(t, []).append(idx)
        o = dict(eng=eng, fn=fn, deps=deps, kind=kind, sig=False, cnt=0)
        if kind == "dma":
            j = self.ndma.get(eng, 0)
            self.ndma[eng] = j + 1
            o["dj"] = j
        self.ops.append(o)
        if kind == "dma":
            self.dmaops.setdefault(eng, []).append(idx)
        else:
            self.lastop[eng] = idx
        return idx

    def op(self, eng, fn, r=(), w=()):
        return self._add(eng, fn, tuple(r), tuple(w), "c")

    def dma(self, q, out, in_, r=(), w=(), **kw):
        return self._add(q, lambda e: e.dma_start(out=out, in_=in_, **kw), tuple(r), tuple(w), "dma")

    def emit(self, final_wait=True):
        nc = self.nc
        ops = self.ops
        engs = ["pe", "act", "dve", "pool", "sp"]
        for o in ops:
            for d in o["deps"]:
                p = ops[d]
                if p["kind"] == "c":
                    if p["eng"] == "pe" and o["eng"] == "pe" and o["kind"] == "c":
                        continue
                    p["sig"] = True
        EP = self.EPOCH
        cnt = {e: 0 for e in engs}
        for o in ops:
            if o["kind"] == "c" and o["sig"]:
                c = cnt[o["eng"]]
                cnt[o["eng"]] = c + 1
                o["ep"] = c // EP
                o["cnt"] = c % EP + 1
        import contextlib
        st = contextlib.ExitStack()
        sems = {}
        for e in engs:
            if e == "sp":
                continue
            for k in range((cnt[e] + EP - 1) // EP + 1):
                sems[(e, k)] = st.enter_context(nc.semaphore("s_%s%d" % (e, k)))
        dsems = {}
        DEP = self.DEPOCH
        for q, n in self.ndma.items():
            for ep in range((n + DEP - 1) // DEP):
                for k in range(self.NSQ):
                    dsems[(q, ep, k)] = st.enter_context(nc.semaphore("d_%s%d_%d" % (q, ep, k)))
        block = st.enter_context(nc.Block())
        by_eng = {e: [o for o in ops if o["eng"] == e] for e in engs}
        NSQ = self.NSQ

        def dkey(q, j):
            ep = j // DEP
            jj = j % DEP
            return (q, ep, jj % NSQ), 16 * (jj // NSQ + 1)

        def run(ename, eng):
            seen = {}

            def wait(key, sem, val):
                if seen.get(key, 0) >= val:
                    return
                seen[key] = val
                eng.wait_ge(sem, val)

            for o in by_eng[ename]:
                need = {}
                for d in o["deps"]:
                    p = ops[d]
                    if p["kind"] == "c":
                        if p["eng"] == "pe" and ename == "pe" and o["kind"] == "c":
                            continue
                        key = (p["eng"], p["ep"])
                        if need.get(key, (None, 0))[1] < p["cnt"]:
                            need[key] = (sems[key], p["cnt"])
                    else:
                        key, val = dkey(p["eng"], p["dj"])
                        if need.get(key, (None, 0))[1] < val:
                            need[key] = (dsems[key], val)
                for key, (sem, val) in need.items():
                    wait(key, sem, val)
                if o["kind"] == "dma":
                    j = o["dj"]
                    if j % DEP >= NSQ:
                        key, val = dkey(ename, j - NSQ)
                        wait(key, dsems[key], val)
                    ins = o["fn"](eng)
                    ins.then_inc(dsems[dkey(ename, j)[0]], 16)
                else:
                    ins = o["fn"](eng)
                    if o["sig"]:
                        ins.then_inc(sems[(ename, o["ep"])], 1)
            if ename == "sp" and final_wait:
                for q, n in self.ndma.items():
                    for j in range(max(0, n - NSQ), n):
                        key, val = dkey(q, j)
                        eng.wait_ge(dsems[key], val)

        @block.tensor
        def _(e):
            run("pe", e)

        @block.scalar
        def _(e):
            run("act", e)

        @block.vector
        def _(e):
            run("dve", e)

        @block.gpsimd
        def _(e):
            run("pool", e)

        @block.sync
        def _(e):
            run("sp", e)

        st.close()

T = 4096
NT = 32
DM = 1024
KC = 8
DIN = 9240
OFF = dict(a_qkv=0, a_beta=1536, a_alpha=1540, a_z=1544, b_qkv=2056, b_f=3592, b_z=3600,
           c_qk=4112, c_v=4624, c_i=5136, c_f=5140, c_o=5144, c_z=5656, gate=6168)
EPS = 1e-6
WNAMES = ["norm_g", "w_in", "conv_a", "a_log", "dt_bias", "norm_a", "fox_f_bias", "conv_c",
          "mlstm_i_bias", "mlstm_f_bias", "norm_c", "proj_a", "proj_b", "proj_c", "w_out", "final_g"]
WSHAPES = dict(norm_g=[4, 1024], w_in=[4, 1024, 9240], conv_a=[4, 4, 1536], a_log=[4, 4], dt_bias=[4, 4],
               norm_a=[4, 512], fox_f_bias=[4, 8], conv_c=[4, 4, 512], mlstm_i_bias=[4, 4],
               mlstm_f_bias=[4, 4], norm_c=[4, 512], proj_a=[4, 512, 1024], proj_b=[4, 512, 1024],
               proj_c=[4, 512, 1024], w_out=[4, 1024, 1024], final_g=[1024])


class SBAlloc:
    def __init__(self, nc, start, end):
        self.nc, self.off, self.end, self.n = nc, start, end, 0

    def alloc(self, name, shape, dt):
        n = 1
        for s in shape[1:]:
            n *= s
        nb = n * (2 if dt == BF16 else 4)
        nb = (nb + 31) // 32 * 32
        self.n += 1
        t = self.nc.alloc_sbuf_tensor_at("%s_%d" % (name, self.n), list(shape), dt, offset=self.off)
        self.off += nb
        assert self.off <= self.end, ("SBUF overflow", name, self.off, self.end)
        return t


def build(nlayers=4, phases="0GABCO", dbg=(), final_norm=True):
    nc = bass.Bass("TRN2", target_bir_lowering=False)
    x_in = nc.dram_tensor("x", [T, DM], F32, kind="ExternalInput").ap()
    W = {n: nc.dram_tensor(n, WSHAPES[n], F32, kind="ExternalInput").ap() for n in WNAMES}
    out = nc.dram_tensor("out", [T, DM], F32, kind="ExternalOutput").ap()
    xs = nc.dram_tensor("xs", [T, DM], F32).ap()
    YTD = nc.dram_tensor("ytd", [3, 4, 128, T], BF16, kind=("ExternalOutput" if dbg else "Internal")).ap()
    MTD = nc.dram_tensor("mtd", [8, 128, T], BF16).ap()
    dbg_out = {}
    P = Prog(nc)
    SB = SBAlloc(nc, 17408, 229376)
    P.bar_tile = SB.alloc("bar", [128, 8], F32)
    uid = [0]
    GV = {}

    def U():
        uid[0] += 1
        return uid[0]

    PSF = [nc.alloc_psum_tensor("psf%d" % i, [128, 512], F32) for i in range(6)]
    PSB = [nc.alloc_psum_tensor("psb%d" % i, [128, 1024], BF16) for i in range(2)]
    psc = [0, 0]

    def psf():
        i = psc[0] % 6
        psc[0] += 1
        return PSF[i], ("psf", i)

    def psb():
        i = psc[1] % 2
        psc[1] += 1
        return PSB[i], ("psb", i)

    IDB = SB.alloc("idb", [128, 128], BF16)
    IDF = SB.alloc("idf", [128, 128], F32)
    UI = SB.alloc("ui", [128, 128], F32)
    SL = SB.alloc("sl", [128, 128], F32)
    UIB = SB.alloc("uib", [128, 128], BF16)
    ONF = SB.alloc("onf", [128, 128], F32)
    P.op("pool", lambda e: e.memset(IDF[:], 0.0), w=["IDF"])
    P.op("pool", lambda e: e.affine_select(IDF[:], IDF[:], [[-1, 128]], ALU.not_equal, 1.0, base=0, channel_multiplier=1), r=["IDF"], w=["IDF"])
    P.op("pool", lambda e: e.tensor_copy(IDB[:], IDF[:]), r=["IDF"], w=["IDB"])
    P.op("pool", lambda e: e.memset(ONF[:], 1.0), w=["ONF"])
    P.op("pool", lambda e: e.affine_select(UI[:], ONF[:], [[1, 128]], ALU.is_ge, 0.0, base=0, channel_multiplier=-1), r=["ONF"], w=["UI"])
    P.op("pool", lambda e: e.affine_select(SL[:], ONF[:], [[-1, 128]], ALU.is_gt, 0.0, base=0, channel_multiplier=1), r=["ONF"], w=["SL"])
    P.op("pool", lambda e: e.tensor_copy(UIB[:], UI[:]), r=["UI"], w=["UIB"])

    HT = SB.alloc("ht", [128, KC, T], BF16)
    GCOL = SB.alloc("gcol", [128, KC], F32)
    BETA = SB.alloc("beta", [128, NT, 4], F32)
    NEGB = SB.alloc("negb", [128, NT, 4], F32)
    GG = SB.alloc("gg", [128, NT, 4], F32)
    EG = SB.alloc("eg", [128, NT, 4], F32)
    ED = SB.alloc("ed", [128, NT, 4], F32)
    GL = SB.alloc("gl", [128, NT, 4], F32)
    BEG = SB.alloc("beg", [128, NT, 4], F32)
    CUM = SB.alloc("cum", [128, NT, 12], F32)
    INCL = SB.alloc("incl", [128, NT, 12], F32)
    WP8 = SB.alloc("wp8", [128, NT, 4], F32)
    THR = SB.alloc("thr", [128, NT, 4], F32)
    GTT = SB.alloc("gtt", [128, NT, 4], F32)
    SMALLP = SB.alloc("smallp", [128, 24], F32)
    NEGA = SB.alloc("nega", [128, 4], F32)
    CONVA = SB.alloc("conva", [128, 12, 4], F32)
    CONVC = SB.alloc("convc", [128, 4, 4], F32)
    NA = SB.alloc("na", [128, 512], F32)
    NCW = SB.alloc("ncw", [128, 512], F32)
    ARENA0 = SB.off

    def arena():
        SB.off = ARENA0

    def dump(name, ap, tok, shape, dt=F32):
        if name in dbg:
            o = nc.dram_tensor("dbg_" + name, list(shape), dt, kind="ExternalOutput").ap()
            dbg_out[name] = o
            P.dma("sp", o, ap, r=[tok])

    rr = [0]

    def cast_eng():
        rr[0] += 1
        return ("pool", "act", "dve", "act")[rr[0] % 4]

    def ecopy(eng, o, i, r, w):
        if eng == "act":
            P.op("act", lambda e: e.copy(o, i), r=r, w=w)
        else:
            P.op(eng, lambda e: e.tensor_copy(o, i), r=r, w=w)

    def load_w(src, kcn, ncols, dst, dtok, WST, scale):
        for c0 in range(0, ncols, 256):
            cw = min(256, ncols - c0)
            b = U() % 2
            st = WST[b]
            P.dma("sp", st[:, 0:kcn, 0:cw], src[:, c0:c0 + cw].rearrange("(k p) c -> p k c", p=128), w=[("WST", b)])
            if scale:
                for kc in range(kcn):
                    eng = cast_eng()
                    if eng == "act":
                        P.op("act", lambda e, kc=kc, st=st, c0=c0, cw=cw: e.mul(dst[:, kc, c0:c0 + cw], st[:, kc, 0:cw], GCOL[:, kc:kc + 1]),
                             r=[("WST", b), "GCOL"], w=[dtok])
                    else:
                        P.op(eng, lambda e, kc=kc, st=st, c0=c0, cw=cw: e.tensor_scalar(dst[:, kc, c0:c0 + cw], st[:, kc, 0:cw], GCOL[:, kc:kc + 1], None, ALU.mult),
                             r=[("WST", b), "GCOL"], w=[dtok])
            else:
                ecopy(cast_eng(), dst[:, 0:kcn, c0:c0 + cw], st[:, 0:kcn, 0:cw], [("WST", b)], [dtok])

    def bc(ap, shape):
        return ap.to_broadcast(list(shape))

    def transpose_out(YZ, t, br):
        pb, ptok = psb()
        for kc in range(4):
            P.op("pe", lambda e, kc=kc, pb=pb: e.transpose(pb[:, kc * 128:(kc + 1) * 128], YZ[:, t, kc * 128:(kc + 1) * 128], IDB[:]),
                 r=[("YZ", t), "IDB"], w=[ptok])
        b = U() % 2
        yst = GV["YST"][b]
        ecopy("act", yst[:], pb[:, 0:512].rearrange("p (k c) -> p k c", k=4), [ptok], [("YST", b)])
        P.dma("sp", YTD[br, :, :, t * 128:(t + 1) * 128].rearrange("k p c -> p k c"), yst[:], r=[("YST", b)], w=[("YTD", br, t)])

    def z_phase(l, YZ, WST, cols, kind, post=None):
        WZ = SB.alloc("wz", [128, KC, 512], BF16)
        col_list = [cols] if kind == "z" else list(cols)
        for ci, c in enumerate(col_list):
            load_w(W["w_in"][l, :, c:c + 512], KC, 512, WZ, "WZ", WST, True)
            for t in range(NT):
                ps, ptok = psf()
                for kc in range(KC):
                    P.op("pe", lambda e, kc=kc, ps=ps, t=t: e.matmul(ps[:, :], HT[:, kc, t * 128:(t + 1) * 128], WZ[:, kc, :], start=(kc == 0), stop=(kc == KC - 1)),
                         r=["HT", "WZ"], w=[ptok])
                if kind == "z":
                    P.op("act", lambda e, ps=ps, t=t: e.activation(YZ[:, t, :], ps[:, :], AF.Silu), r=[ptok], w=[("YZ", t)])
                    if post is not None:
                        P.op("pool", lambda e, t=t: e.tensor_tensor(YZ[:, t, :], YZ[:, t, :], post[:], ALU.mult), r=[("YZ", t), "NPOST"], w=[("YZ", t)])
                elif ci == 0:
                    P.op("act", lambda e, ps=ps, t=t: e.activation(YZ[:, t, :], ps[:, :], AF.Sigmoid), r=[ptok], w=[("YZ", t)])
                else:
                    b = U() % 2
                    P.op("act", lambda e, ps=ps, b=b: e.activation(GV["ZT"][b][:], ps[:, :], AF.Silu), r=[ptok], w=[("ZT", b)])
                    P.op("dve", lambda e, t=t, b=b: e.tensor_tensor(YZ[:, t, :], YZ[:, t, :], GV["ZT"][b][:], ALU.mult), r=[("ZT", b), ("YZ", t)], w=[("YZ", t)])
                    if post is not None:
                        P.op("pool", lambda e, t=t: e.tensor_tensor(YZ[:, t, :], YZ[:, t, :], post[:], ALU.mult), r=[("YZ", t), "NPOST"], w=[("YZ", t)])

    class _Stop(Exception):
        pass
    import os
    astop = int(os.environ.get('ASTOP', '99'))

    def chk(k):
        if astop == k:
            raise _Stop()

    def layers():
      for l in range(nlayers):
          P.barrier()
          arena()
          def phase_0():
              P.dma("sp", GCOL[:], W["norm_g"][l].rearrange("(k p) -> p k", p=128), w=["GCOL"], allow_slow_non_contiguous=True)
              for i, (nm, n) in enumerate([("dt_bias", 4), ("a_log", 4), ("fox_f_bias", 8), ("mlstm_i_bias", 4), ("mlstm_f_bias", 4)]):
                  o0 = [0, 4, 8, 16, 20][i]
                  P.dma("sp", SMALLP[:, o0:o0 + n], W[nm][l].partition_broadcast(128), w=["SMALLP"])
              for j in range(4):
                  P.dma("sp", CONVA[:, :, j], W["conv_a"][l, j].rearrange("(c p) -> p c", p=128), w=["CONVA"], allow_slow_non_contiguous=True)
                  P.dma("sp", CONVC[:, :, j], W["conv_c"][l, j].rearrange("(c p) -> p c", p=128), w=["CONVC"], allow_slow_non_contiguous=True)
              P.dma("sp", NA[:], W["norm_a"][l].partition_broadcast(128), w=["NA", "NPOST"])
              P.dma("sp", NCW[:], W["norm_c"][l].partition_broadcast(128), w=["NCW", "NPOST"])
              P.op("act", lambda e: e.activation(NEGA[:], SMALLP[:, 4:8], AF.Exp), r=["SMALLP"], w=["NEGA"])
              P.op("dve", lambda e: e.tensor_scalar(NEGA[:], NEGA[:], -1.0, None, ALU.mult), r=["NEGA"], w=["NEGA"])
              XT = [SB.alloc("xt", [128, DM], F32) for _ in range(2)]
              XN = [SB.alloc("xn", [128, DM], BF16) for _ in range(2)]
              SQJ = SB.alloc("sqj", [128, DM], F32)
              SSQ = SB.alloc("ssq", [128, NT], F32)
              RSTD = SB.alloc("rstd", [128, NT], F32)
              src = x_in if l == 0 else xs
              P.op("pool", lambda e: e.memset(SSQ[:], 0.0), w=["SSQ"])
              for t in range(NT):
                  b = t % 2
                  P.dma("sp", XT[b][:], src[t * 128:(t + 1) * 128, :], r=[("XS", t)], w=[("XT", b)])
                  P.op("act", lambda e, b=b, t=t: e.activation(SQJ[:], XT[b][:], AF.Square, accum_out=SSQ[:, t:t + 1]), r=[("XT", b), "SSQ"], w=["SQJ", ("SSQ", t)])
                  P.op("act", lambda e, t=t: e.activation(RSTD[:, t:t + 1], SSQ[:, t:t + 1], AF.Sqrt, bias=EPS, scale=1.0 / DM), r=[("SSQ", t)], w=[("RSTD", t)])
                  P.op("dve", lambda e, t=t: e.reciprocal(RSTD[:, t:t + 1], RSTD[:, t:t + 1]), r=[("RSTD", t)], w=[("RSTD", t)])
                  P.op("dve", lambda e, b=b, t=t: e.tensor_scalar(XN[b][:], XT[b][:], RSTD[:, t:t + 1], None, ALU.mult), r=[("XT", b), ("RSTD", t)], w=[("XN", b)])
                  pb, ptok = psb()
                  for kc in range(KC):
                      P.op("pe", lambda e, kc=kc, pb=pb, b=b: e.transpose(pb[:, kc * 128:(kc + 1) * 128], XN[b][:, kc * 128:(kc + 1) * 128], IDB[:]),
                           r=[("XN", b), "IDB"], w=[ptok])
                  ecopy(("act", "dve")[t % 2], HT[:, :, t * 128:(t + 1) * 128], pb[:].rearrange("p (k c) -> p k c", k=KC), [ptok], ["HT"])
              dump("ht", HT[:, 0, :], "HT", [128, T], BF16)
          if "0" in phases:
              phase_0()

          def phase_G():
              P.barrier()
              arena()
              WST = [SB.alloc("wst", [128, KC, 256], F32) for _ in range(2)]
              WSM = SB.alloc("wsm", [128, KC, 24], BF16)
              GP = SB.alloc("gp", [128, NT, 24], F32)
              for i, (nm, n) in enumerate([("a_beta", 4), ("a_alpha", 4), ("b_f", 8), ("c_i", 4), ("c_f", 4)]):
                  o0 = [0, 4, 8, 16, 20][i]
                  P.dma("sp", WST[0][:, :, o0:o0 + n], W["w_in"][l, :, OFF[nm]:OFF[nm] + n].rearrange("(k p) c -> p k c", p=128), w=[("WST", 0)])
              for kc in range(KC):
                  P.op("pool", lambda e, kc=kc: e.tensor_scalar(WSM[:, kc, :], WST[0][:, kc, 0:24], GCOL[:, kc:kc + 1], None, ALU.mult), r=[("WST", 0), "GCOL"], w=["WSM"])
              for t0 in (0, 16):
                  ps, ptok = psf()
                  for t in range(t0, t0 + 16):
                      for kc in range(KC):
                          P.op("pe", lambda e, kc=kc, ps=ps, t=t, t0=t0: e.matmul(ps[:, (t - t0) * 24:(t - t0 + 1) * 24], HT[:, kc, t * 128:(t + 1) * 128], WSM[:, kc, :], start=(kc == 0), stop=(kc == KC - 1)),
                               r=["HT", "WSM"], w=[ptok])
                  P.op("dve", lambda e, ps=ps, t0=t0: e.tensor_copy(GP[:, t0:t0 + 16, :], ps[:, 0:384].rearrange("p (t c) -> p t c", c=24)), r=[ptok], w=["GP"])
              X16 = SB.alloc("x16", [128, NT, 16], F32)
              NX = SB.alloc("nx", [128, NT, 16], F32)
              SPL = SB.alloc("spl", [128, NT, 16], F32)
              IP = SB.alloc("ip", [128, NT, 4], F32)
              LF12 = SB.alloc("lf12", [128, NT, 12], F32)
              TOTA = SB.alloc("tota", [128, NT, 12], F32)
              SCA = [SB.alloc("sca", [128, NT, 12], F32) for _ in range(2)]
              TMP4 = SB.alloc("tmp4", [128, NT, 4], F32)
              MX4 = SB.alloc("mx4", [128, 4], F32)
              MST = SB.alloc("mst", [128, 4], F32)
              MXT = SB.alloc("mxt", [4, 4], F32)
              MXD = SB.alloc("mxd", [4, 4], F32)

              def spb(o0, n):
                  return bc(SMALLP[:, o0:o0 + n].unsqueeze(1), [128, NT, n])
              P.op("dve", lambda e: e.tensor_tensor(X16[:, :, 0:4], GP[:, :, 4:8], spb(0, 4), ALU.add), r=["GP", "SMALLP"], w=["X16"])
              P.op("dve", lambda e: e.scalar_tensor_tensor(X16[:, :, 4:12], GP[:, :, 8:16], -1.0, spb(8, 8), ALU.mult, ALU.subtract), r=["GP", "SMALLP"], w=["X16"])
              P.op("dve", lambda e: e.scalar_tensor_tensor(X16[:, :, 12:16], GP[:, :, 20:24], -1.0, spb(20, 4), ALU.mult, ALU.subtract), r=["GP", "SMALLP"], w=["X16"])
              P.op("act", lambda e: e.mul(NX[:], X16[:], -1.0), r=["X16"], w=["NX"])
              P.op("dve", lambda e: e.tensor_tensor(NX[:], NX[:], X16[:], ALU.min), r=["NX", "X16"], w=["NX"])
              P.op("act", lambda e: e.activation(NX[:], NX[:], AF.Exp), r=["NX"], w=["NX"])
              P.op("act", lambda e: e.activation(NX[:], NX[:], AF.Ln, bias=1.0, scale=1.0), r=["NX"], w=["NX"])
              P.op("dve", lambda e: e.tensor_scalar(SPL[:], X16[:], 0.0, None, ALU.max), r=["X16"], w=["SPL"])
              P.op("dve", lambda e: e.tensor_tensor(SPL[:], SPL[:], NX[:], ALU.add), r=["SPL", "NX"], w=["SPL"])
              P.op("dve", lambda e: e.tensor_tensor(GG[:], SPL[:, :, 0:4], bc(NEGA[:].unsqueeze(1), [128, NT, 4]), ALU.mult), r=["SPL", "NEGA"], w=["GG"])
              P.op("dve", lambda e: e.tensor_scalar(LF12[:], SPL[:, :, 4:16], -1.0, None, ALU.mult), r=["SPL"], w=["LF12"])
              P.op("act", lambda e: e.activation(BETA[:], GP[:, :, 0:4], AF.Sigmoid), r=["GP"], w=["BETA"])
              P.op("dve", lambda e: e.tensor_scalar(NEGB[:], BETA[:], -1.0, None, ALU.mult), r=["BETA"], w=["NEGB"])
              P.op("dve", lambda e: e.tensor_tensor(IP[:], GP[:, :, 16:20], spb(16, 4), ALU.add), r=["GP", "SMALLP"], w=["IP"])
              ps, ptok = psf()
              P.op("pe", lambda e, ps=ps: e.matmul(ps[:, 0:128], UI[:], GG[:].rearrange("p t c -> p (t c)"), start=True, stop=True), r=["UI", "GG"], w=[ptok])
              P.op("pe", lambda e, ps=ps: e.matmul(ps[:, 128:256], ONF[:], GG[:].rearrange("p t c -> p (t c)"), start=True, stop=True), r=["ONF", "GG"], w=[ptok])
              GCf = SB.alloc("gcf", [128, 128], F32)
              GEf = SB.alloc("gef", [128, 128], F32)
              P.op("dve", lambda e, ps=ps: e.tensor_copy(GCf[:], ps[:, 0:128]), r=[ptok], w=["GCf"])
              P.op("dve", lambda e, ps=ps: e.tensor_copy(GEf[:], ps[:, 128:256]), r=[ptok], w=["GEf"])
              fl = lambda a: a[:].rearrange("p t c -> p (t c)")
              P.op("act", lambda e: e.activation(fl(EG), GCf[:], AF.Exp), r=["GCf"], w=["EG"])
              P.op("act", lambda e: e.activation(fl(GL), GEf[:], AF.Exp), r=["GEf"], w=["GL"])
              P.op("dve", lambda e: e.tensor_tensor(GEf[:], GEf[:], GCf[:], ALU.subtract), r=["GEf", "GCf"], w=["GEf"])
              P.op("act", lambda e: e.activation(fl(ED), GEf[:], AF.Exp), r=["GEf"], w=["ED"])
              P.op("dve", lambda e: e.tensor_tensor(BEG[:], BETA[:], EG[:], ALU.mult), r=["BETA", "EG"], w=["BEG"])
              ps, ptok = psf()
              P.op("pe", lambda e, ps=ps: e.matmul(ps[:, 0:384], ONF[:], fl(LF12), start=True, stop=True), r=["ONF", "LF12"], w=[ptok])
              P.op("dve", lambda e, ps=ps: e.tensor_copy(fl(TOTA), ps[:, 0:384]), r=[ptok], w=["TOTA"])
              P.op("dve", lambda e: e.tensor_copy(SCA[0][:], TOTA[:]), r=["TOTA"], w=[("SCA", 0)])
              cur = 0
              for s in (1, 2, 4, 8, 16):
                  nxt = 1 - cur
                  P.op("dve", lambda e, s=s, cur=cur, nxt=nxt: e.tensor_tensor(SCA[nxt][:, s:, :], SCA[cur][:, s:, :], SCA[cur][:, 0:NT - s, :], ALU.add), r=[("SCA", cur)], w=[("SCA", nxt)])
                  P.op("pool", lambda e, s=s, cur=cur, nxt=nxt: e.tensor_copy(SCA[nxt][:, 0:s, :], SCA[cur][:, 0:s, :]), r=[("SCA", cur)], w=[("SCA", nxt)])
                  cur = nxt
              P.op("dve", lambda e, cur=cur: e.tensor_copy(INCL[:], SCA[cur][:]), r=[("SCA", cur)], w=["INCL"])
              P.op("dve", lambda e: e.tensor_tensor(CUM[:], INCL[:], TOTA[:], ALU.subtract), r=["INCL", "TOTA"], w=["CUM"])
              ps, ptok = psf()
              P.op("pe", lambda e, ps=ps: e.matmul(ps[:, 0:384], UI[:], fl(LF12), start=True, stop=True), r=["UI", "LF12"], w=[ptok])
              P.op("dve", lambda e, ps=ps: e.tensor_tensor(fl(CUM), fl(CUM), ps[:, 0:384], ALU.add), r=[ptok, "CUM"], w=["CUM"])
              P.op("dve", lambda e: e.tensor_reduce(MX4[:], IP[:].rearrange("p t c -> p c t"), AX.X, ALU.max), r=["IP"], w=["MX4"])
              ps, ptok = psf()
              P.op("pe", lambda e, ps=ps: e.transpose(ps[0:4, 0:128], MX4[:], IDF[:]), r=["MX4", "IDF"], w=[ptok])
              P.op("dve", lambda e, ps=ps: e.tensor_reduce(MXT[:, 0:1], ps[0:4, 0:128], AX.X, ALU.max), r=[ptok], w=["MXT"])
              P.op("dve", lambda e: e.tensor_scalar(MXD[:], IDF[0:4, 0:4], MXT[:, 0:1], None, ALU.mult), r=["MXT", "IDF"], w=["MXD"])
              ps, ptok = psf()
              P.op("pe", lambda e, ps=ps: e.matmul(ps[:, 0:4], ONF[0:4, :], MXD[:], start=True, stop=True), r=["ONF", "MXD"], w=[ptok])
              P.op("dve", lambda e, ps=ps: e.tensor_copy(MST[:], ps[:, 0:4]), r=[ptok], w=["MST"])
              mstb = lambda: bc(MST[:].unsqueeze(1), [128, NT, 4])
              P.op("dve", lambda e: e.tensor_tensor(TMP4[:], INCL[:, :, 8:12], CUM[:, :, 8:12], ALU.subtract), r=["INCL", "CUM"], w=["TMP4"])
              P.op("dve", lambda e: e.tensor_tensor(TMP4[:], TMP4[:], mstb(), ALU.subtract), r=["TMP4", "MST"], w=["TMP4"])
              P.op("act", lambda e: e.activation(THR[:], TMP4[:], AF.Exp), r=["TMP4"], w=["THR"])
              P.op("dve", lambda e: e.tensor_tensor(TMP4[:], TMP4[:], IP[:], ALU.add), r=["TMP4", "IP"], w=["TMP4"])
              P.op("act", lambda e: e.activation(WP8[:], TMP4[:], AF.Exp), r=["TMP4"], w=["WP8"])
              P.op("dve", lambda e: e.tensor_scalar(WP8[:], WP8[:], 0.125, None, ALU.mult), r=["WP8"], w=["WP8"])
              P.op("act", lambda e: e.activation(GTT[:], TOTA[:, :, 8:12], AF.Exp), r=["TOTA"], w=["GTT"])
              dump("beta", BETA[:], "BETA", [128, NT, 4])
              dump("gg", GG[:], "GG", [128, NT, 4])
              dump("cum", CUM[:], "CUM", [128, NT, 12])
              dump("incl", INCL[:], "INCL", [128, NT, 12])
              dump("lf12", LF12[:], "LF12", [128, NT, 12])
          if "G" in phases:
              phase_G()

          def phase_A():
              P.barrier()
              arena()
              YZ = SB.alloc("yz", [128, NT, 512], BF16)
              YST = GV["YST"] = [SB.alloc("yst", [128, 4, 128], BF16) for _ in range(2)]
              WA = SB.alloc("wa", [128, KC, 1536], BF16)
              CT = SB.alloc("ct", [128, 12, 256], BF16)
              RAW = [SB.alloc("raw", [128, 259], F32) for _ in range(2)]
              ACC = [SB.alloc("acc", [128, 256], F32) for _ in range(2)]
              HALO = SB.alloc("halo", [128, 12, 3], F32)
              S32 = SB.alloc("s32", [128, 4, 128], F32)
              SBF = SB.alloc("sbf", [128, 4, 128], BF16)
              a1 = SB.off
              WST = [SB.alloc("wst", [128, KC, 256], F32) for _ in range(2)]
              a0 = SB.off
              z_phase(l, YZ, WST, OFF["a_z"], "z", post=NA)
              SB.off = a0
              load_w(W["w_in"][l, :, 0:1536], KC, 1536, WA, "WA", WST, True)
              P.barrier()
              SB.off = a1

              def mkset(sid):
                  B = dict(sid=sid)
                  for nm, shp, dt in (("SS8", [128, 12], F32), ("RS8", [128, 12], F32), ("JUNK", [128, 128], F32),
                                      ("QKN", [128, 8, 128], BF16), ("VB", [128, 4, 128], BF16), ("KBG", [128, 4, 128], BF16),
                                      ("KD", [128, 4, 128], BF16), ("QKT", [128, 8, 128], BF16), ("UG", [128, 4, 128], F32),
                                      ("E1", [128, 4, 128], F32), ("E2", [128, 4, 128], F32), ("QKM", [128, 4, 128], BF16),
                                      ("MP0", [128, 4, 128], F32), ("MP1", [128, 4, 128], F32), ("MPT0", [128, 4, 128], F32),
                                      ("MPT1", [128, 4, 128], F32), ("XTB", [128, 4, 128], BF16), ("NWT", [128, 4, 128], BF16),
                                      ("VNB", [128, 4, 128], BF16)):
                      B[nm] = SB.alloc(nm.lower(), shp, dt)
                  return B
              SETS = [mkset(0), mkset(1)]
              P.op("pool", lambda e: e.memset(HALO[:], 0.0), w=["HALO"])
              P.op("pool", lambda e: e.memset(S32[:], 0.0), w=["S32"])
              P.op("pool", lambda e: e.memset(SBF[:], 0.0), w=["SBF"])
              f4 = lambda a: a[:].rearrange("p h c -> p (h c)")

              def hb(ap2):
                  return bc(ap2.unsqueeze(2), [128, 4, 128])

              def conv_block(blk):
                  for ch in range(12):
                      ps, ptok = psf()
                      for kc in range(KC):
                          P.op("pe", lambda e, kc=kc, ps=ps, ch=ch, blk=blk: e.matmul(ps[:, 0:256], WA[:, kc, ch * 128:(ch + 1) * 128], HT[:, kc, blk * 256:(blk + 1) * 256], start=(kc == 0), stop=(kc == KC - 1)),
                               r=["WA", "HT"], w=[ptok])
                      b = ch % 2
                      raw, acc = RAW[b], ACC[b]
                      P.op("pool", lambda e, raw=raw, ch=ch: e.tensor_copy(raw[:, 0:3], HALO[:, ch, :]), r=["HALO"], w=[("RAW", b)])
                      P.op("act", lambda e, raw=raw, ps=ps: e.copy(raw[:, 3:259], ps[:, 0:256]), r=[ptok], w=[("RAW", b)])
                      P.op("pool", lambda e, raw=raw, ch=ch: e.tensor_copy(HALO[:, ch, :], raw[:, 256:259]), r=[("RAW", b)], w=["HALO"])
                      P.op("dve", lambda e, raw=raw, acc=acc, ch=ch: e.tensor_scalar(acc[:], raw[:, 3:259], CONVA[:, ch, 3:4], None, ALU.mult), r=[("RAW", b), "CONVA"], w=[("ACC", b)])
                      for j in (2, 1, 0):
                          P.op("dve", lambda e, raw=raw, acc=acc, ch=ch, j=j: e.scalar_tensor_tensor(acc[:], raw[:, j:j + 256], CONVA[:, ch, j:j + 1], acc[:], ALU.mult, ALU.add),
                               r=[("RAW", b), ("ACC", b), "CONVA"], w=[("ACC", b)])
                      P.op("act", lambda e, acc=acc, ch=ch: e.activation(CT[:, ch, :], acc[:], AF.Silu), r=[("ACC", b)], w=[("CT", ch)])

              def gdn_tile(t, tt, B):
                  sid = B["sid"]
                  T_ = lambda n: (n, sid)
                  SS8, RS8, JUNK, QKN, VB, KBG, KD, QKT, UG, E1, E2, QKM, XTB, NWT, VNB = (B[k] for k in ("SS8", "RS8", "JUNK", "QKN", "VB", "KBG", "KD", "QKT", "UG", "E1", "E2", "QKM", "XTB", "NWT", "VNB"))
                  MP = [B["MP0"], B["MP1"]]
                  MPT = [B["MPT0"], B["MPT1"]]
                  XTT, T1, O32, MN = UG, E1, E2, MP[0]
                  cs = slice(tt * 128, (tt + 1) * 128)
                  pA, tA = psb()
                  for ch in range(8):
                      P.op("pe", lambda e, ch=ch: e.transpose(pA[:, ch * 128:(ch + 1) * 128], CT[:, ch, cs], IDB[:]), r=[("CT", ch), "IDB"], w=[tA])
                  P.op("pool", lambda e: e.memset(SS8[:], 0.0), w=[T_("SS8")])
                  for j in range(8):
                      P.op("act", lambda e, j=j: e.activation(JUNK[:], pA[:, j * 128:(j + 1) * 128], AF.Square, accum_out=SS8[:, j:j + 1]), r=[tA, T_("SS8")], w=[T_("JUNK"), T_("SS8")])
                  P.op("act", lambda e: e.activation(RS8[:, 0:4], SS8[:, 0:4], AF.Sqrt, bias=128.0 * EPS, scale=128.0), r=[T_("SS8")], w=[T_("RS8")])
                  P.op("act", lambda e: e.activation(RS8[:, 4:8], SS8[:, 4:8], AF.Sqrt, bias=EPS, scale=1.0), r=[T_("SS8")], w=[T_("RS8")])
                  P.op("dve", lambda e: e.reciprocal(RS8[:, 0:8], RS8[:, 0:8]), r=[T_("RS8")], w=[T_("RS8")])
                  P.op("dve", lambda e: e.tensor_tensor(QKN[:], pA[:].rearrange("p (h c) -> p h c", h=8), bc(RS8[:, 0:8].unsqueeze(2), [128, 8, 128]), ALU.mult), r=[tA, T_("RS8")], w=[T_("QKN")])
                  yield
                  pV, tV = psb()
                  for ch in range(4):
                      P.op("pe", lambda e, ch=ch: e.transpose(pV[:, ch * 128:(ch + 1) * 128], CT[:, 8 + ch, cs], IDB[:]), r=[("CT", 8 + ch), "IDB"], w=[tV])
                  P.op("dve", lambda e: e.tensor_tensor(VB[:], pV[:, 0:512].rearrange("p (h c) -> p h c", h=4), hb(BETA[:, t, :]), ALU.mult), r=[tV, "BETA"], w=[T_("VB")])
                  P.op("pool", lambda e: e.tensor_tensor(KBG[:], QKN[:, 4:8, :], hb(BEG[:, t, :]), ALU.mult), r=[T_("QKN"), "BEG"], w=[T_("KBG")])
                  P.op("pool", lambda e: e.tensor_tensor(KD[:], QKN[:, 4:8, :], hb(ED[:, t, :]), ALU.mult), r=[T_("QKN"), "ED"], w=[T_("KD")])
                  yield
                  pT, tT = psb()
                  for j in range(8):
                      P.op("pe", lambda e, j=j: e.transpose(pT[:, j * 128:(j + 1) * 128], QKN[:, j, :], IDB[:]), r=[T_("QKN"), "IDB"], w=[tT])
                  ecopy("act", QKT[:], pT[:].rearrange("p (h c) -> p h c", h=8), [tT], [T_("QKT")])
                  yield
                  for h in range(4):
                      P.op("act", lambda e, h=h: e.mul(UG[:, h, :], UI[:], GG[:, t, h:h + 1]), r=["UI", "GG"], w=[T_("UG")])
                  pD1, tD1 = psf()
                  pD2, tD2 = psf()
                  for h in range(4):
                      P.op("pe", lambda e, h=h: e.matmul(pD1[:, h * 128:(h + 1) * 128], UG[:, h, :], SL[:], start=True, stop=True), r=[T_("UG"), "SL"], w=[tD1])
                      P.op("pe", lambda e, h=h: e.matmul(pD2[:, h * 128:(h + 1) * 128], SL[:], UG[:, h, :], start=True, stop=True), r=[T_("UG"), "SL"], w=[tD2])
                  P.op("act", lambda e: e.activation(f4(E1), pD1[:, :], AF.Exp), r=[tD1], w=[T_("E1")])
                  P.op("act", lambda e: e.activation(f4(E2), pD2[:, :], AF.Exp), r=[tD2], w=[T_("E2")])
                  yield
                  P.op("dve", lambda e: e.tensor_tensor(E1[:], E1[:], bc(SL[:].unsqueeze(1), [128, 4, 128]), ALU.mult), r=[T_("E1"), "SL"], w=[T_("E1")])
                  P.op("pool", lambda e: e.tensor_tensor(E2[:], E2[:], bc(UI[:].unsqueeze(1), [128, 4, 128]), ALU.mult), r=[T_("E2"), "UI"], w=[T_("E2")])
                  pK, tK = psf()
                  pQ, tQ = psf()
                  for h in range(4):
                      P.op("pe", lambda e, h=h: e.matmul(pK[:, h * 128:(h + 1) * 128], QKT[:, 4 + h, :], QKT[:, 4 + h, :], start=True, stop=True), r=[T_("QKT")], w=[tK])
                      P.op("pe", lambda e, h=h: e.matmul(pQ[:, h * 128:(h + 1) * 128], QKT[:, 4 + h, :], QKT[:, h, :], start=True, stop=True), r=[T_("QKT")], w=[tQ])
                  P.op("dve", lambda e: e.tensor_tensor(f4(E1), pK[:, :], f4(E1), ALU.mult), r=[tK, T_("E1")], w=[T_("E1")])
                  for h in range(4):
                      P.op("act", lambda e, h=h: e.mul(MN[:, h, :], E1[:, h, :], NEGB[:, t, h:h + 1]), r=[T_("E1"), "NEGB"], w=[T_("MP0")])
                  P.op("dve", lambda e: e.tensor_tensor(f4(QKM), pQ[:, :], f4(E2), ALU.mult), r=[tQ, T_("E2")], w=[T_("QKM")])
                  yield
                  pM, tM = psf()
                  for h in range(4):
                      P.op("pe", lambda e, h=h: e.matmul(pM[:, h * 128:(h + 1) * 128], MN[:, h, :], IDF[:], start=True, stop=True), r=[T_("MP0"), "IDF"], w=[tM])
                  ecopy("act", f4(MPT[0]), pM[:, :], [tM], [T_("MPT0")])
                  P.op("dve", lambda e: e.tensor_tensor(XTT[:], MPT[0][:], bc(IDF[:].unsqueeze(1), [128, 4, 128]), ALU.add), r=[T_("MPT0"), "IDF"], w=[T_("UG")])
                  yield
                  cur = 0
                  for lev in range(1, 7):
                      nxt = 1 - cur
                      last = lev == 6
                      p1, t1 = psf()
                      for h in range(4):
                          P.op("pe", lambda e, h=h, p1=p1, cur=cur: e.matmul(p1[:, h * 128:(h + 1) * 128], MPT[cur][:, h, :], MP[cur][:, h, :], start=True, stop=True), r=[T_("MP%d" % cur), T_("MPT%d" % cur)], w=[t1])
                      if not last:
                          p2, t2 = psf()
                          for h in range(4):
                              P.op("pe", lambda e, h=h, p2=p2, cur=cur: e.matmul(p2[:, h * 128:(h + 1) * 128], MP[cur][:, h, :], MPT[cur][:, h, :], start=True, stop=True), r=[T_("MP%d" % cur), T_("MPT%d" % cur)], w=[t2])
                      ecopy("act", f4(MP[nxt]), p1[:, :], [t1], [T_("MP%d" % nxt)])
                      if not last:
                          ecopy("dve", f4(MPT[nxt]), p2[:, :], [t2], [T_("MPT%d" % nxt)])
                      yield
                      p3, t3 = psf()
                      for h in range(4):
                          P.op("pe", lambda e, h=h, p3=p3, nxt=nxt: e.matmul(p3[:, h * 128:(h + 1) * 128], MP[nxt][:, h, :], XTT[:, h, :], start=True, stop=True), r=[T_("MP%d" % nxt), T_("UG")], w=[t3])
                      P.op("dve", lambda e, p3=p3: e.tensor_tensor(f4(XTT), f4(XTT), p3[:, :], ALU.add), r=[t3, T_("UG")], w=[T_("UG")])
                      cur = nxt
                      yield
                  ecopy("act", XTB[:], XTT[:], [T_("UG")], [T_("XTB")])
                  pW, tW = psf()
                  for h in range(4):
                      P.op("pe", lambda e, h=h: e.matmul(pW[:, h * 128:(h + 1) * 128], KBG[:, h, :], XTB[:, h, :], start=True, stop=True), r=[T_("KBG"), T_("XTB")], w=[tW])
                  P.op("act", lambda e: e.mul(f4(NWT), pW[:, :], -1.0), r=[tW], w=[T_("NWT")])
                  yield
                  pN, tN = psf()
                  for h in range(4):
                      P.op("pe", lambda e, h=h: e.matmul(pN[:, h * 128:(h + 1) * 128], XTB[:, h, :], VB[:, h, :], start=True, stop=False), r=[T_("XTB"), T_("VB")], w=[tN])
                      P.op("pe", lambda e, h=h: e.matmul(pN[:, h * 128:(h + 1) * 128], NWT[:, h, :], SBF[:, h, :], start=False, stop=True), r=[T_("NWT"), "SBF"], w=[tN])
                  ecopy("dve", f4(VNB), pN[:, :], [tN], [T_("VNB")])
                  p1, t1 = psf()
                  p2, t2 = psf()
                  for h in range(4):
                      P.op("pe", lambda e, h=h: e.matmul(p1[:, h * 128:(h + 1) * 128], QKT[:, h, :], SBF[:, h, :], start=True, stop=True), r=[T_("QKT"), "SBF"], w=[t1])
                      P.op("pe", lambda e, h=h: e.matmul(p2[:, h * 128:(h + 1) * 128], QKM[:, h, :], VNB[:, h, :], start=True, stop=True), r=[T_("QKM"), T_("VNB")], w=[t2])
                  pS, tS = psf()
                  for h in range(4):
                      P.op("pe", lambda e, h=h: e.matmul(pS[:, h * 128:(h + 1) * 128], KD[:, h, :], VNB[:, h, :], start=True, stop=True), r=[T_("KD"), T_("VNB")], w=[tS])
                  for h in range(4):
                      P.op("dve", lambda e, h=h: e.scalar_tensor_tensor(S32[:, h, :], S32[:, h, :], GL[:, t, h:h + 1], pS[:, h * 128:(h + 1) * 128], ALU.mult, ALU.add), r=[tS, "S32", "GL"], w=["S32"])
                  ecopy("act", SBF[:], S32[:], ["S32"], ["SBF"])
                  P.op("dve", lambda e: e.tensor_tensor(T1[:], p1[:, :].rearrange("p (h c) -> p h c", h=4), hb(EG[:, t, :]), ALU.mult), r=[t1, "EG"], w=[T_("E1")])
                  P.op("dve", lambda e: e.tensor_tensor(f4(O32), f4(T1), p2[:, :], ALU.add), r=[t2, T_("E1")], w=[T_("E2")])
                  yield
                  for h in range(4):
                      P.op("act", lambda e, h=h: e.activation(JUNK[:], O32[:, h, :], AF.Square, accum_out=SS8[:, 8 + h:9 + h]), r=[T_("E2"), T_("SS8")], w=[T_("JUNK"), T_("SS8")])
                  P.op("act", lambda e: e.activation(RS8[:, 8:12], SS8[:, 8:12], AF.Sqrt, bias=EPS, scale=1.0 / 128), r=[T_("SS8")], w=[T_("RS8")])
                  P.op("dve", lambda e: e.reciprocal(RS8[:, 8:12], RS8[:, 8:12]), r=[T_("RS8")], w=[T_("RS8")])
                  P.op("dve", lambda e: e.tensor_tensor(O32[:], O32[:], hb(RS8[:, 8:12]), ALU.mult), r=[T_("E2"), T_("RS8")], w=[T_("E2")])
                  P.op("dve", lambda e: e.tensor_tensor(YZ[:, t, :], f4(O32), YZ[:, t, :], ALU.mult), r=[T_("E2"), ("YZ", t)], w=[("YZ", t)])
                  transpose_out(YZ, t, 0)

              def run_rr(gens):
                  gens = list(gens)
                  while gens:
                      for g in list(gens):
                          try:
                              next(g)
                          except StopIteration:
                              gens.remove(g)

              for blk in range(16):
                  conv_block(blk)
                  run_rr([gdn_tile(blk * 2, 0, SETS[0]), gdn_tile(blk * 2 + 1, 1, SETS[1])])
          if "A" in phases:
              phase_A()

          def phase_B():
              P.barrier()
              arena()
              YZ = SB.alloc("yz", [128, NT, 512], BF16)
              WST = [SB.alloc("wst", [128, KC, 256], F32) for _ in range(2)]
              YST = GV["YST"] = [SB.alloc("yst", [128, 4, 128], BF16) for _ in range(2)]
              a0 = SB.off
              z_phase(l, YZ, WST, OFF["b_z"], "z")
              SB.off = a0
              P.barrier()
              WB = SB.alloc("wb", [128, KC, 384], BF16)
              QT = SB.alloc("qt", [128, T], BF16)
              KT = SB.alloc("kt", [128, T], BF16)
              VA = SB.alloc("va", [128, NT, 2, 65], BF16)
              BK = SB.alloc("bk", [128, NT, 16], F32)
              PT = [SB.alloc("pt", [128, 512], BF16) for _ in range(2)]
              RL = SB.alloc("rl", [128, 4], F32)
              OT = SB.alloc("ot", [128, 4, 64], F32)
              P.op("pool", lambda e: e.memset(VA[:], 1.0), w=["VA"])
              for pj in range(4):
                  for part in range(3):
                      c0 = OFF["b_qkv"] + part * 512 + pj * 128
                      load_w(W["w_in"][l, :, c0:c0 + 128], KC, 128, WB[:, :, part * 128:(part + 1) * 128], "WB", WST, True)
                  for blk in range(8):
                      for part, dst in ((0, QT), (1, KT)):
                          ps, ptok = psf()
                          for kc in range(KC):
                              P.op("pe", lambda e, kc=kc, ps=ps, part=part, blk=blk: e.matmul(ps[:, :], WB[:, kc, part * 128:(part + 1) * 128], HT[:, kc, blk * 512:(blk + 1) * 512], start=(kc == 0), stop=(kc == KC - 1)),
                                   r=["WB", "HT"], w=[ptok])
                          if part == 0:
                              P.op("act", lambda e, ps=ps, blk=blk: e.mul(QT[:, blk * 512:(blk + 1) * 512], ps[:, :], 0.125), r=[ptok], w=["QT"])
                          else:
                              P.op("dve", lambda e, ps=ps, blk=blk: e.tensor_copy(KT[:, blk * 512:(blk + 1) * 512], ps[:, :]), r=[ptok], w=["KT"])
                  for t in range(NT):
                      ps, ptok = psf()
                      for kc in range(KC):
                          P.op("pe", lambda e, kc=kc, ps=ps, t=t: e.matmul(ps[:, 0:128], HT[:, kc, t * 128:(t + 1) * 128], WB[:, kc, 256:384], start=(kc == 0), stop=(kc == KC - 1)),
                               r=["WB", "HT"], w=[ptok])
                      P.op("dve", lambda e, ps=ps, t=t: e.tensor_copy(VA[:, t, :, 0:64], ps[:, 0:128].rearrange("p (h c) -> p h c", h=2)), r=[ptok], w=["VA"])
                  for hh in range(2):
                      h = 2 * pj + hh
                      hp = hh * 64
                      P.op("dve", lambda e, h=h: e.tensor_tensor(BK[:], bc(INCL[:, 1:NT:2, h:h + 1].rearrange("p t c -> p c t"), [128, NT, 16]),
                                                                  bc(CUM[:, :, h:h + 1], [128, NT, 16]), ALU.subtract), r=["INCL", "CUM"], w=["BK"])
                      for qb in range(8):
                          po, otok = PSF[4 + qb % 2], ("psf", 4 + qb % 2)
                          nk = 4 * qb + 4

                          def emitS(kt, qb=qb):
                              c0 = max(kt - 4 * qb, 0) * 128
                              sci = kt % 4
                              ps, stok = PSF[sci], ("psf", sci)
                              P.op("pe", lambda e, ps=ps, kt=kt, qb=qb, c0=c0, hp=hp: e.matmul(ps[:, c0:512], KT[hp:hp + 64, kt * 128:(kt + 1) * 128], QT[hp:hp + 64, qb * 512 + c0:(qb + 1) * 512], start=True, stop=True),
                                   r=["KT", "QT"], w=[stok])
                          LA = 2
                          for kt in range(min(LA, nk)):
                              emitS(kt)
                          for kt in range(nk):
                              if kt + LA < nk:
                                  emitS(kt + LA)
                              i = max(kt - 4 * qb, 0)
                              sci = kt % 4
                              ps, stok = PSF[sci], ("psf", sci)
                              pb_ = kt % 2
                              pt = PT[pb_]
                              for g2 in range(2):
                                  ca = max(g2 * 256, i * 128)
                                  cb = (g2 + 1) * 256
                                  if ca >= cb:
                                      continue
                                  gi = 2 * qb + g2
                                  P.op("act", lambda e, ps=ps, pt=pt, ca=ca, cb=cb, kt=kt, gi=gi: e.activation(pt[:, ca:cb], ps[:, ca:cb], AF.Exp, bias=BK[:, kt, gi:gi + 1], scale=1.0),
                                       r=[stok, "BK"], w=[("PT", pb_, jq) for jq in range(ca // 128, cb // 128)])
                              for jq in range(i, 4):
                                  qt = 4 * qb + jq
                                  if kt == qt:
                                      P.op("pool", lambda e, pt=pt, jq=jq: e.tensor_tensor(pt[:, jq * 128:(jq + 1) * 128], pt[:, jq * 128:(jq + 1) * 128], UIB[:], ALU.mult),
                                           r=[("PT", pb_, jq), "UIB"], w=[("PT", pb_, jq)])
                                  P.op("pe", lambda e, po=po, pt=pt, jq=jq, kt=kt, qt=qt, hh=hh: e.matmul(po[:, jq * 65:(jq + 1) * 65], pt[:, jq * 128:(jq + 1) * 128], VA[:, kt, hh, :], start=(kt == 0 and jq == 0), stop=(kt == qt), skip_group_check=True),
                                       r=[("PT", pb_, jq), "VA"], w=[otok])
                          pov = po[:, 0:260].rearrange("p (j c) -> p j c", c=65)
                          P.op("dve", lambda e, pov=pov: e.reciprocal(RL[:], pov[:, :, 64]), r=[otok], w=["RL"])
                          P.op("dve", lambda e, pov=pov: e.tensor_tensor(OT[:], pov[:, :, 0:64], bc(RL[:].unsqueeze(2), [128, 4, 64]), ALU.mult), r=[otok, "RL"], w=["OT"])
                          for jq in range(4):
                              qt = 4 * qb + jq
                              P.op("pool", lambda e, jq=jq, qt=qt, h=h: e.tensor_tensor(YZ[:, qt, h * 64:(h + 1) * 64], OT[:, jq, :], YZ[:, qt, h * 64:(h + 1) * 64], ALU.mult),
                                   r=["OT", ("YZ", qt)], w=[("YZ", qt)])
              for t in range(NT):
                  transpose_out(YZ, t, 1)
          if "B" in phases:
              phase_B()
          def phase_C():
              P.barrier()
              arena()
              YZ = SB.alloc("yz", [128, NT, 512], BF16)
              WST = [SB.alloc("wst", [128, KC, 256], F32) for _ in range(2)]
              YST = GV["YST"] = [SB.alloc("yst", [128, 4, 128], BF16) for _ in range(2)]
              ZT = GV["ZT"] = [SB.alloc("zt", [128, 512], BF16) for _ in range(2)]
              a0 = SB.off
              z_phase(l, YZ, WST, (OFF["c_o"], OFF["c_z"]), "oz", post=NCW)
              SB.off = a0
              P.barrier()
              chk(20)
              WC = SB.alloc("wc", [128, KC, 1024], BF16)
              load_w(W["w_in"][l, :, OFF["c_qk"]:OFF["c_qk"] + 1024], KC, 1024, WC, "WC", WST, True)
              CTC = SB.alloc("ctc", [128, 4, 512], BF16)
              RAW = [SB.alloc("raw", [128, 515], F32) for _ in range(2)]
              ACC = [SB.alloc("acc", [128, 512], F32) for _ in range(2)]
              HALOC = SB.alloc("halo", [128, 4, 3], F32)
              VAC = SB.alloc("vac", [128, 4, 129], BF16)
              KW = SB.alloc("kw", [128, 4, 64], BF16)
              KWT = SB.alloc("kwt", [128, 2, 128], BF16)
              AM = SB.alloc("am", [128, 4, 128], BF16)
              CS32 = SB.alloc("cs32", [128, 2, 129], F32)
              CG32 = SB.alloc("cg32", [128, 2, 129], F32)
              CGB = SB.alloc("cgb", [128, 2, 129], BF16)
              DEN = SB.alloc("den", [128, 4], F32)
              HH = SB.alloc("hh", [128, 4, 128], F32)
              JUNKC = SB.alloc("junk", [128, 128], F32)
              SS4 = SB.alloc("ss4", [128, 4], F32)
              P.op("pool", lambda e: e.memset(HALOC[:], 0.0), w=["HALOC"])
              P.op("pool", lambda e: e.memset(CS32[:], 0.0), w=["CS32"])
              P.op("pool", lambda e: e.memset(VAC[:], 1.0), w=["VAC"])
              f4 = lambda a: a[:].rearrange("p h c -> p (h c)")
              for blk in range(8):
                  for ch in range(4):
                      ps, ptok = psf()
                      for kc in range(KC):
                          P.op("pe", lambda e, kc=kc, ps=ps, ch=ch, blk=blk: e.matmul(ps[:, :], WC[:, kc, ch * 128:(ch + 1) * 128], HT[:, kc, blk * 512:(blk + 1) * 512], start=(kc == 0), stop=(kc == KC - 1)),
                               r=["WC", "HT"], w=[ptok])
                      b = U() % 2
                      raw, acc = RAW[b], ACC[b]
                      P.op("pool", lambda e, raw=raw, ch=ch: e.tensor_copy(raw[:, 0:3], HALOC[:, ch, :]), r=["HALOC"], w=[("RAW", b)])
                      P.op("act", lambda e, raw=raw, ps=ps: e.copy(raw[:, 3:515], ps[:, :]), r=[ptok], w=[("RAW", b)])
                      P.op("pool", lambda e, raw=raw, ch=ch: e.tensor_copy(HALOC[:, ch, :], raw[:, 512:515]), r=[("RAW", b)], w=["HALOC"])
                      P.op("dve", lambda e, raw=raw, acc=acc, ch=ch: e.tensor_scalar(acc[:], raw[:, 3:515], CONVC[:, ch, 3:4], None, ALU.mult), r=[("RAW", b), "CONVC"], w=[("ACC", b)])
                      for j in (2, 1, 0):
                          P.op("dve", lambda e, raw=raw, acc=acc, ch=ch, j=j: e.scalar_tensor_tensor(acc[:], raw[:, j:j + 512], CONVC[:, ch, j:j + 1], acc[:], ALU.mult, ALU.add),
                               r=[("RAW", b), ("ACC", b), "CONVC"], w=[("ACC", b)])
                      P.op("act", lambda e, acc=acc, ch=ch: e.activation(CTC[:, ch, :], acc[:], AF.Silu), r=[("ACC", b)], w=[("CTC", ch)])
                  for tt in range(4):
                      t = blk * 4 + tt
                      cs = slice(tt * 128, (tt + 1) * 128)
                      chk(21)
                      ps, ptok = psf()
                      for kc in range(KC):
                          P.op("pe", lambda e, kc=kc, ps=ps, t=t: e.matmul(ps[:, :], HT[:, kc, t * 128:(t + 1) * 128], WC[:, kc, 512:1024], start=(kc == 0), stop=(kc == KC - 1)),
                               r=["WC", "HT"], w=[ptok])
                      P.op("act", lambda e, ps=ps: e.copy(VAC[:, :, 0:128], ps[:, :].rearrange("p (h c) -> p h c", h=4)), r=[ptok], w=["VAC"])
                      pk, tk = psb()
                      for p_ in range(2):
                          P.op("pe", lambda e, p_=p_, pk=pk, cs=cs: e.transpose(pk[:, p_ * 128:(p_ + 1) * 128], CTC[:, 2 + p_, cs], IDB[:]), r=[("CTC", 2 + p_), "IDB"], w=[tk])
                      P.op("dve", lambda e, pk=pk, t=t: e.tensor_tensor(KW[:], pk[:, 0:256].rearrange("p (h c) -> p h c", h=4), bc(WP8[:, t, :].unsqueeze(2), [128, 4, 64]), ALU.mult), r=[tk, "WP8"], w=["KW"])
                      pt_, tt_ = psb()
                      for p_ in range(2):
                          P.op("pe", lambda e, p_=p_, pt_=pt_: e.transpose(pt_[:, p_ * 128:(p_ + 1) * 128], KW[:, 2 * p_:2 * p_ + 2, :].rearrange("p h c -> p (h c)"), IDB[:]), r=["KW", "IDB"], w=[tt_])
                      ecopy("act", KWT[:], pt_[:, 0:256].rearrange("p (h c) -> p h c", h=2), [tt_], ["KWT"])
                      chk(22)
                      paL = [psf(), psf()]
                      for h in range(4):
                          p_, hp = h // 2, (h % 2) * 64
                          pa, ta = paL[h % 2]
                          P.op("pe", lambda e, h=h, p_=p_, hp=hp, pa=pa, cs=cs: e.matmul(pa[:, p_ * 128:(p_ + 1) * 128], KWT[hp:hp + 64, p_, :], CTC[hp:hp + 64, p_, cs], start=True, stop=True),
                               r=["KWT", ("CTC", p_)], w=[ta])
                      for h in range(4):
                          pa, ta = paL[h % 2]
                          P.op("dve", lambda e, pa=pa, h=h: e.tensor_tensor(AM[:, h, :], pa[:, (h // 2) * 128:(h // 2 + 1) * 128], UI[:], ALU.mult), r=[ta, "UI"], w=["AM"])
                      chk(23)
                      for h in range(4):
                          p_, hp = h // 2, (h % 2) * 64
                          P.op("pool", lambda e, h=h, p_=p_, hp=hp, t=t: e.tensor_scalar(CG32[hp:hp + 64, p_, :], CS32[hp:hp + 64, p_, :], GTT[hp:hp + 64, t, h:h + 1], None, ALU.mult),
                               r=["CS32", "GTT"], w=["CG32"])
                      ecopy("act", CGB[:], CG32[:], ["CG32"], ["CGB"])
                      chk(24)
                      pn = []
                      for g in range(2):
                          pn.append(psf())
                      for h in range(4):
                          p_, hp = h // 2, (h % 2) * 64
                          png, tng = pn[h % 2]
                          o0 = (h // 2) * 129
                          P.op("pe", lambda e, h=h, png=png, o0=o0: e.matmul(png[:, o0:o0 + 129], AM[:, h, :], VAC[:, h, :], start=True, stop=False), r=["AM", "VAC"], w=[tng])
                          P.op("pe", lambda e, h=h, p_=p_, hp=hp, png=png, o0=o0, cs=cs: e.matmul(png[:, o0:o0 + 129], CTC[hp:hp + 64, p_, cs], CGB[hp:hp + 64, p_, :], start=False, stop=True), r=[("CTC", p_), "CGB"], w=[tng])
                      chk(25)
                      psu = []
                      for g in range(2):
                          psu.append(psf())
                      for h in range(4):
                          p_, hp = h // 2, (h % 2) * 64
                          pg, tg = psu[h // 2]
                          o0 = (h % 2) * 129
                          P.op("pe", lambda e, h=h, p_=p_, pg=pg, o0=o0: e.matmul(pg[:, o0:o0 + 129], KW[:, 2 * p_:2 * p_ + 2, :].rearrange("p h c -> p (h c)"), VAC[:, h, :], start=True, stop=True), r=["KW", "VAC"], w=[tg])
                      for h in range(4):
                          p_, hp = h // 2, (h % 2) * 64
                          pg, tg = psu[h // 2]
                          o0 = (h % 2) * 129
                          P.op("dve", lambda e, h=h, p_=p_, hp=hp, pg=pg, o0=o0: e.tensor_tensor(CS32[hp:hp + 64, p_, :], CG32[hp:hp + 64, p_, :], pg[hp:hp + 64, o0:o0 + 129], ALU.add), r=[tg, "CG32"], w=["CS32"])
                      chk(26)
                      for g in range(2):
                          png, tng = pn[g]
                          pv = png[:, 0:258].rearrange("p (h c) -> p h c", h=2)
                          P.op("dve", lambda e, pv=pv, g=g: e.tensor_copy(DEN[:, g:g + 3:2], pv[:, :, 128]), r=[tng], w=["DEN"])
                      P.op("dve", lambda e: e.scalar_tensor_tensor(DEN[:], DEN[:], -1.0, DEN[:], ALU.mult, ALU.max), r=["DEN"], w=["DEN"])
                      P.op("dve", lambda e, t=t: e.tensor_tensor(DEN[:], DEN[:], THR[:, t, :], ALU.max), r=["DEN", "THR"], w=["DEN"])
                      P.op("dve", lambda e: e.reciprocal(DEN[:], DEN[:]), r=["DEN"], w=["DEN"])
                      for g in range(2):
                          png, tng = pn[g]
                          pv = png[:, 0:258].rearrange("p (h c) -> p h c", h=2)
                          P.op("dve", lambda e, pv=pv, g=g: e.tensor_tensor(HH[:, g:g + 3:2, :], pv[:, :, 0:128], bc(DEN[:, g:g + 3:2].unsqueeze(2), [128, 2, 128]), ALU.mult), r=[tng, "DEN"], w=["HH"])
                      chk(27)
                      P.op("pool", lambda e: e.memset(SS4[:], 0.0), w=["SS4"])
                      for h in range(4):
                          P.op("act", lambda e, h=h: e.activation(JUNKC[:], HH[:, h, :], AF.Square, accum_out=SS4[:, h:h + 1]), r=["HH", "SS4"], w=["JUNKC", "SS4"])
                      P.op("act", lambda e: e.activation(SS4[:], SS4[:], AF.Sqrt, bias=EPS, scale=1.0 / 128), r=["SS4"], w=["SS4"])
                      P.op("dve", lambda e: e.reciprocal(SS4[:], SS4[:]), r=["SS4"], w=["SS4"])
                      P.op("dve", lambda e: e.tensor_tensor(HH[:], HH[:], bc(SS4[:].unsqueeze(2), [128, 4, 128]), ALU.mult), r=["HH", "SS4"], w=["HH"])
                      P.op("dve", lambda e, t=t: e.tensor_tensor(YZ[:, t, :], f4(HH), YZ[:, t, :], ALU.mult), r=["HH", ("YZ", t)], w=[("YZ", t)])
                      transpose_out(YZ, t, 2)
          if "C" in phases:
              phase_C()
          def phase_O():
              P.barrier()
              arena()
              WG = SB.alloc("wg", [128, KC, 3072], BF16)
              PJ = SB.alloc("pj", [128, 12, 1024], BF16)
              a1 = SB.off
              WST = [SB.alloc("wst", [128, KC, 256], F32) for _ in range(2)]
              for br in range(3):
                  c0 = OFF["gate"] + br * 1024
                  load_w(W["w_in"][l, :, c0:c0 + 1024], KC, 1024, WG[:, :, br * 1024:(br + 1) * 1024], "WG", WST, True)
                  pw = W[("proj_a", "proj_b", "proj_c")[br]]
                  load_w(pw[l], 4, 1024, PJ[:, br * 4:(br + 1) * 4, :], "PJ", WST, False)
              P.barrier()
              SB.off = a1
              YTB = [SB.alloc("ytb", [128, 12, 512], BF16) for _ in range(2)]
              SG = [SB.alloc("sg", [128, 512], F32) for _ in range(2)]
              MA = [SB.alloc("ma", [128, 512], F32) for _ in range(2)]
              MB = [SB.alloc("mb", [128, 8, 512], BF16) for _ in range(2)]
              for blk in range(8):
                  yb = blk % 2
                  for br in range(3):
                      P.dma("sp", YTB[yb][:, br * 4:(br + 1) * 4, :], YTD[br, :, :, blk * 512:(blk + 1) * 512].rearrange("k p c -> p k c"),
                            r=[("YTD", br, t) for t in range(blk * 4, blk * 4 + 4)], w=[("YTB", yb, br)])
                  mbb = blk % 2
                  for fc in range(8):
                      mab = fc % 2
                      for br in range(3):
                          pp, tp = psf()
                          for kc in range(4):
                              P.op("pe", lambda e, kc=kc, pp=pp, br=br, fc=fc, yb=yb: e.matmul(pp[:, :], PJ[:, br * 4 + kc, fc * 128:(fc + 1) * 128], YTB[yb][:, br * 4 + kc, :], start=(kc == 0), stop=(kc == 3)),
                                   r=["PJ", ("YTB", yb, br)], w=[tp])
                          pg, tg = psf()
                          for kc in range(KC):
                              P.op("pe", lambda e, kc=kc, pg=pg, br=br, fc=fc, blk=blk: e.matmul(pg[:, :], WG[:, kc, br * 1024 + fc * 128:br * 1024 + (fc + 1) * 128], HT[:, kc, blk * 512:(blk + 1) * 512], start=(kc == 0), stop=(kc == KC - 1)),
                                   r=["WG", "HT"], w=[tg])
                          sb_ = br % 2
                          P.op("act", lambda e, pg=pg, sb_=sb_: e.activation(SG[sb_][:], pg[:, :], AF.Sigmoid), r=[tg], w=[("SG", sb_)])
                          if br == 0:
                              P.op("dve", lambda e, pp=pp, sb_=sb_, mab=mab: e.tensor_tensor(MA[mab][:], SG[sb_][:], pp[:, :], ALU.mult), r=[tp, ("SG", sb_)], w=[("MA", mab)])
                          else:
                              P.op("dve", lambda e, pp=pp, sb_=sb_: e.tensor_tensor(SG[sb_][:], SG[sb_][:], pp[:, :], ALU.mult), r=[tp, ("SG", sb_)], w=[("SG", sb_)])
                              if br == 1:
                                  P.op("pool", lambda e, sb_=sb_, mab=mab: e.tensor_tensor(MA[mab][:], MA[mab][:], SG[sb_][:], ALU.add), r=[("MA", mab), ("SG", sb_)], w=[("MA", mab)])
                              else:
                                  P.op("pool", lambda e, sb_=sb_, mab=mab, mbb=mbb, fc=fc: e.tensor_tensor(MB[mbb][:, fc, :], MA[mab][:], SG[sb_][:], ALU.add), r=[("MA", mab), ("SG", sb_)], w=[("MB", mbb)])
                  P.dma("sp", MTD[:, :, blk * 512:(blk + 1) * 512].rearrange("k p c -> p k c"), MB[mbb][:], r=[("MB", mbb)], w=[("MTD", blk)])
              P.barrier()
              arena()
              WST = [SB.alloc("wst", [128, KC, 256], F32) for _ in range(2)]
              WO = SB.alloc("wo", [128, KC, 1024], BF16)
              load_w(W["w_out"][l], KC, 1024, WO, "WO", WST, False)
              MT = [SB.alloc("mt", [128, KC, 128], BF16) for _ in range(2)]
              XR = [SB.alloc("xr", [128, DM], F32) for _ in range(2)]
              XO = [SB.alloc("xo", [128, DM], F32) for _ in range(2)]
              last = (l == nlayers - 1) and final_norm
              if last:
                  FG = SB.alloc("fg", [128, DM], F32)
                  SQJ2 = SB.alloc("sqj", [128, DM], F32)
                  FS = SB.alloc("fs", [128, NT], F32)
                  P.dma("sp", FG[:], W["final_g"].partition_broadcast(128), w=["FG"])
                  P.op("pool", lambda e: e.memset(FS[:], 0.0), w=["FS"])
              src = x_in if l == 0 else xs
              for t in range(NT):
                  b = t % 2
                  P.dma("sp", MT[b][:], MTD[:, :, t * 128:(t + 1) * 128].rearrange("k p c -> p k c"), r=[("MTD", t // 4)], w=[("MT", b)])
                  P.dma("sp", XR[b][:], src[t * 128:(t + 1) * 128, :], r=[("XS", t)], w=[("XR", b)])
                  for half in range(2):
                      ps, ptok = psf()
                      for kc in range(KC):
                          P.op("pe", lambda e, kc=kc, ps=ps, b=b, half=half: e.matmul(ps[:, :], MT[b][:, kc, :], WO[:, kc, half * 512:(half + 1) * 512], start=(kc == 0), stop=(kc == KC - 1)),
                               r=[("MT", b), "WO"], w=[ptok])
                      P.op("dve", lambda e, ps=ps, b=b, half=half: e.tensor_tensor(XO[b][:, half * 512:(half + 1) * 512], XR[b][:, half * 512:(half + 1) * 512], ps[:, :], ALU.add), r=[ptok, ("XR", b)], w=[("XO", b)])
                  if not last:
                      P.dma("sp", xs[t * 128:(t + 1) * 128, :], XO[b][:], r=[("XO", b)], w=[("XS", t)])
                      if "x1" in dbg:
                          P.dma("sp", out[t * 128:(t + 1) * 128, :], XO[b][:], r=[("XO", b)])
                  else:
                      P.op("act", lambda e, b=b, t=t: e.activation(SQJ2[:], XO[b][:], AF.Square, accum_out=FS[:, t:t + 1]), r=[("XO", b), "FS"], w=["SQJ2", ("FS", t)])
                      P.op("act", lambda e, t=t: e.activation(FS[:, t:t + 1], FS[:, t:t + 1], AF.Sqrt, bias=EPS, scale=1.0 / DM), r=[("FS", t)], w=[("FS", t)])
                      P.op("dve", lambda e, t=t: e.reciprocal(FS[:, t:t + 1], FS[:, t:t + 1]), r=[("FS", t)], w=[("FS", t)])
                      P.op("dve", lambda e, b=b, t=t: e.scalar_tensor_tensor(XO[b][:], XO[b][:], FS[:, t:t + 1], FG[:], ALU.mult, ALU.mult), r=[("XO", b), ("FS", t), "FG"], w=[("XO", b)])
                      P.dma("sp", out[t * 128:(t + 1) * 128, :], XO[b][:], r=[("XO", b)], w=[("OUT", t)])
          if "O" in phases:
              phase_O()

    try:
        layers()
    except _Stop:
        pass
    P.emit()
    return nc, dbg_out


def kernel(**inputs):
    nc, _ = build()
    x = np.ascontiguousarray(inputs["x"], dtype=np.float32)
    wmap = {n: np.ascontiguousarray(inputs[n], dtype=np.float32) for n in WNAMES}
    in_maps = []
    for c in range(8):
        m = dict(wmap)
        m["x"] = x[c]
        in_maps.append(m)
    res = run_bass_kernel_spmd(nc, in_maps, core_ids=list(range(8)))
    return np.stack([np.asarray(r["out"]) for r in res.results], axis=0).astype(np.float32)
```

```python
import numpy as np
import concourse.bass as bass
import concourse.mybir as mybir
from concourse.bass_utils import run_bass_kernel_spmd

F32 = mybir.dt.float32
BF16 = mybir.dt.bfloat16
ALU = mybir.AluOpType
AF = mybir.ActivationFunctionType
AX = mybir.AxisListType


class Prog:
    NSQ = 6
    EPOCH = 30000
    DEPOCH = 6 * 3000

    def __init__(self, nc):
        self.nc = nc
        self.ops = []
        self.last_w = {}
        self.readers = {}
        self.ndma = {}
        self.bar = None
        self.lastop = {}
        self.dmaops = {}

    def barrier(self):
        deps = set(self.lastop.values())
        for q, lst in self.dmaops.items():
            deps.update(lst[-self.NSQ:])
        nc = self.nc
        idx = self._add("pool", lambda e: e.memset(self.bar_tile[:], 0.0), (), (), "c")
        self.ops[idx]["deps"].update(deps)
        self.bar = idx

    def _add(self, eng, fn, r, w, kind):
        idx = len(self.ops)
        deps = set()
        for t in r:
            if t in self.last_w:
                deps.add(self.last_w[t])
        for t in w:
            if t in self.last_w:
                deps.add(self.last_w[t])
            for x in self.readers.get(t, ()):
                deps.add(x)
        deps.discard(idx)
        if self.bar is not None:
            deps.add(self.bar)
        for t in w:
            self.last_w[t] = idx
            self.readers[t] = []
        for t in r:
            if t not in w:
                self.readers.setdefault(t, []).append(idx)
        o = dict(eng=eng, fn=fn, deps=deps, kind=kind, sig=False, cnt=0)
        if kind == "dma":
            j = self.ndma.get(eng, 0)
            self.ndma[eng] = j + 1
            o["dj"] = j
        self.ops.append(o)
        if kind == "dma":
            self.dmaops.setdefault(eng, []).append(idx)
        else:
            self.lastop[eng] = idx
        return idx

    def op(self, eng, fn, r=(), w=()):
        return self._add(eng, fn, tuple(r), tuple(w), "c")

    def dma(self, q, out, in_, r=(), w=(), **kw):
        return self._add(q, lambda e: e.dma_start(out=out, in_=in_, **kw), tuple(r), tuple(w), "dma")

    def emit(self, final_wait=True):
        nc = self.nc
        ops = self.ops
        engs = ["pe", "act", "dve", "pool", "sp"]
        for o in ops:
            for d in o["deps"]:
                p = ops[d]
                if p["kind"] == "c":
                    if p["eng"] == "pe" and o["eng"] == "pe" and o["kind"] == "c":
                        continue
                    p["sig"] = True
        EP = self.EPOCH
        cnt = {e: 0 for e in engs}
        for o in ops:
            if o["kind"] == "c" and o["sig"]:
                c = cnt[o["eng"]]
                cnt[o["eng"]] = c + 1
                o["ep"] = c // EP
                o["cnt"] = c % EP + 1
        import contextlib
        st = contextlib.ExitStack()
        sems = {}
        for e in engs:
            if e == "sp":
                continue
            for k in range((cnt[e] + EP - 1) // EP + 1):
                sems[(e, k)] = st.enter_context(nc.semaphore("s_%s%d" % (e, k)))
        dsems = {}
        DEP = self.DEPOCH
        for q, n in self.ndma.items():
            for ep in range((n + DEP - 1) // DEP):
                for k in range(self.NSQ):
                    dsems[(q, ep, k)] = st.enter_context(nc.semaphore("d_%s%d_%d" % (q, ep, k)))
        block = st.enter_context(nc.Block())
        by_eng = {e: [o for o in ops if o["eng"] == e] for e in engs}
        NSQ = self.NSQ

        def dkey(q, j):
            ep = j // DEP
            jj = j % DEP
            return (q, ep, jj % NSQ), 16 * (jj // NSQ + 1)

        def run(ename, eng):
            seen = {}

            def wait(key, sem, val):
                if seen.get(key, 0) >= val:
                    return
                seen[key] = val
                eng.wait_ge(sem, val)

            for o in by_eng[ename]:
                need = {}
                for d in o["deps"]:
                    p = ops[d]
                    if p["kind"] == "c":
                        if p["eng"] == "pe" and ename == "pe" and o["kind"] == "c":
                            continue
                        key = (p["eng"], p["ep"])
                        if need.get(key, (None, 0))[1] < p["cnt"]:
                            need[key] = (sems[key], p["cnt"])
                    else:
                        key, val = dkey(p["eng"], p["dj"])
                        if need.get(key, (None, 0))[1] < val:
                            need[key] = (dsems[key], val)
                for key, (sem, val) in need.items():
                    wait(key, sem, val)
                if o["kind"] == "dma":
                    j = o["dj"]
                    if j % DEP >= NSQ:
                        key, val = dkey(ename, j - NSQ)
                        wait(key, dsems[key], val)
                    ins = o["fn"](eng)
                    ins.then_inc(dsems[dkey(ename, j)[0]], 16)
                else:
                    ins = o["fn"](eng)
                    if o["sig"]:
                        ins.then_inc(sems[(ename, o["ep"])], 1)
            if ename == "sp" and final_wait:
                for q, n in self.ndma.items():
                    for j in range(max(0, n - NSQ), n):
                        key, val = dkey(q, j)
                        eng.wait_ge(dsems[key], val)

        @block.tensor
        def _(e):
            run("pe", e)

        @block.scalar
        def _(e):
            run("act", e)

        @block.vector
        def _(e):
            run("dve", e)

        @block.gpsimd
        def _(e):
            run("pool", e)

        @block.sync
        def _(e):
            run("sp", e)

        st.close()

T = 4096
NT = 32
DM = 1024
KC = 8
DIN = 9240
OFF = dict(a_qkv=0, a_beta=1536, a_alpha=1540, a_z=1544, b_qkv=2056, b_f=3592, b_z=3600,
           c_qk=4112, c_v=4624, c_i=5136, c_f=5140, c_o=5144, c_z=5656, gate=6168)
EPS = 1e-6
WNAMES = ["norm_g", "w_in", "conv_a", "a_log", "dt_bias", "norm_a", "fox_f_bias", "conv_c",
          "mlstm_i_bias", "mlstm_f_bias", "norm_c", "proj_a", "proj_b", "proj_c", "w_out", "final_g"]
WSHAPES = dict(norm_g=[4, 1024], w_in=[4, 1024, 9240], conv_a=[4, 4, 1536], a_log=[4, 4], dt_bias=[4, 4],
               norm_a=[4, 512], fox_f_bias=[4, 8], conv_c=[4, 4, 512], mlstm_i_bias=[4, 4],
               mlstm_f_bias=[4, 4], norm_c=[4, 512], proj_a=[4, 512, 1024], proj_b=[4, 512, 1024],
               proj_c=[4, 512, 1024], w_out=[4, 1024, 1024], final_g=[1024])


class SBAlloc:
    def __init__(self, nc, start, end):
        self.nc, self.off, self.end, self.n = nc, start, end, 0

    def alloc(self, name, shape, dt):
        n = 1
        for s in shape[1:]:
            n *= s
        nb = n * (2 if dt == BF16 else 4)
        nb = (nb + 31) // 32 * 32
        self.n += 1
        t = self.nc.alloc_sbuf_tensor_at("%s_%d" % (name, self.n), list(shape), dt, offset=self.off)
        self.off += nb
        assert self.off <= self.end, ("SBUF overflow", name, self.off, self.end)
        return t


def build(nlayers=4, phases="0GABCO", dbg=(), final_norm=True):
    nc = bass.Bass("TRN2", target_bir_lowering=False)
    x_in = nc.dram_tensor("x", [T, DM], F32, kind="ExternalInput").ap()
    W = {n: nc.dram_tensor(n, WSHAPES[n], F32, kind="ExternalInput").ap() for n in WNAMES}
    out = nc.dram_tensor("out", [T, DM], F32, kind="ExternalOutput").ap()
    xs = nc.dram_tensor("xs", [T, DM], F32).ap()
    YTD = nc.dram_tensor("ytd", [3, 4, 128, T], BF16, kind=("ExternalOutput" if dbg else "Internal")).ap()
    MTD = nc.dram_tensor("mtd", [8, 128, T], BF16).ap()
    dbg_out = {}
    P = Prog(nc)
    SB = SBAlloc(nc, 17408, 229376)
    P.bar_tile = SB.alloc("bar", [128, 8], F32)
    uid = [0]
    GV = {}

    def U():
        uid[0] += 1
        return uid[0]

    PSF = [nc.alloc_psum_tensor("psf%d" % i, [128, 512], F32) for i in range(6)]
    PSB = [nc.alloc_psum_tensor("psb%d" % i, [128, 1024], BF16) for i in range(2)]
    psc = [0, 0]

    def psf():
        i = psc[0] % 6
        psc[0] += 1
        return PSF[i], ("psf", i)

    def psb():
        i = psc[1] % 2
        psc[1] += 1
        return PSB[i], ("psb", i)

    IDB = SB.alloc("idb", [128, 128], BF16)
    IDF = SB.alloc("idf", [128, 128], F32)
    UI = SB.alloc("ui", [128, 128], F32)
    SL = SB.alloc("sl", [128, 128], F32)
    UIB = SB.alloc("uib", [128, 128], BF16)
    ONF = SB.alloc("onf", [128, 128], F32)
    P.op("pool", lambda e: e.memset(IDF[:], 0.0), w=["IDF"])
    P.op("pool", lambda e: e.affine_select(IDF[:], IDF[:], [[-1, 128]], ALU.not_equal, 1.0, base=0, channel_multiplier=1), r=["IDF"], w=["IDF"])
    P.op("pool", lambda e: e.tensor_copy(IDB[:], IDF[:]), r=["IDF"], w=["IDB"])
    P.op("pool", lambda e: e.memset(ONF[:], 1.0), w=["ONF"])
    P.op("pool", lambda e: e.affine_select(UI[:], ONF[:], [[1, 128]], ALU.is_ge, 0.0, base=0, channel_multiplier=-1), r=["ONF"], w=["UI"])
    P.op("pool", lambda e: e.affine_select(SL[:], ONF[:], [[-1, 128]], ALU.is_gt, 0.0, base=0, channel_multiplier=1), r=["ONF"], w=["SL"])
    P.op("pool", lambda e: e.tensor_copy(UIB[:], UI[:]), r=["UI"], w=["UIB"])

    HT = SB.alloc("ht", [128, KC, T], BF16)
    GCOL = SB.alloc("gcol", [128, KC], F32)
    BETA = SB.alloc("beta", [128, NT, 4], F32)
    NEGB = SB.alloc("negb", [128, NT, 4], F32)
    GG = SB.alloc("gg", [128, NT, 4], F32)
    EG = SB.alloc("eg", [128, NT, 4], F32)
    ED = SB.alloc("ed", [128, NT, 4], F32)
    GL = SB.alloc("gl", [128, NT, 4], F32)
    BEG = SB.alloc("beg", [128, NT, 4], F32)
    CUM = SB.alloc("cum", [128, NT, 12], F32)
    INCL = SB.alloc("incl", [128, NT, 12], F32)
    WP8 = SB.alloc("wp8", [128, NT, 4], F32)
    THR = SB.alloc("thr", [128, NT, 4], F32)
    GTT = SB.alloc("gtt", [128, NT, 4], F32)
    SMALLP = SB.alloc("smallp", [128, 24], F32)
    NEGA = SB.alloc("nega", [128, 4], F32)
    CONVA = SB.alloc("conva", [128, 12, 4], F32)
    CONVC = SB.alloc("convc", [128, 4, 4], F32)
    NA = SB.alloc("na", [128, 512], F32)
    NCW = SB.alloc("ncw", [128, 512], F32)
    ARENA0 = SB.off

    def arena():
        SB.off = ARENA0

    def dump(name, ap, tok, shape, dt=F32):
        if name in dbg:
            o = nc.dram_tensor("dbg_" + name, list(shape), dt, kind="ExternalOutput").ap()
            dbg_out[name] = o
            P.dma("sp", o, ap, r=[tok])

    rr = [0]

    def cast_eng():
        rr[0] += 1
        return ("pool", "act", "dve", "act")[rr[0] % 4]

    def ecopy(eng, o, i, r, w):
        if eng == "act":
            P.op("act", lambda e: e.copy(o, i), r=r, w=w)
        else:
            P.op(eng, lambda e: e.tensor_copy(o, i), r=r, w=w)

    def run_rr(gens):
        gens = list(gens)
        while gens:
            for g in list(gens):
                try:
                    next(g)
                except StopIteration:
                    gens.remove(g)

    def load_w(src, kcn, ncols, dst, dtok, WST, scale):
        for _ in load_w_gen(src, kcn, ncols, dst, dtok, WST, scale):
            pass

    def load_w_gen(src, kcn, ncols, dst, dtok, WST, scale):
        for c0 in range(0, ncols, 256):
            cw = min(256, ncols - c0)
            b = (c0 // 256) % len(WST)
            st = WST[b]
            P.dma("sp", st[:, 0:kcn, 0:cw], src[:, c0:c0 + cw].rearrange("(k p) c -> p k c", p=128), w=[("WST", b)])
            if scale:
                for kc in range(kcn):
                    eng = cast_eng()
                    if eng == "act":
                        P.op("act", lambda e, kc=kc, st=st, c0=c0, cw=cw: e.mul(dst[:, kc, c0:c0 + cw], st[:, kc, 0:cw], GCOL[:, kc:kc + 1]),
                             r=[("WST", b), "GCOL"], w=[dtok])
                    else:
                        P.op(eng, lambda e, kc=kc, st=st, c0=c0, cw=cw: e.tensor_scalar(dst[:, kc, c0:c0 + cw], st[:, kc, 0:cw], GCOL[:, kc:kc + 1], None, ALU.mult),
                             r=[("WST", b), "GCOL"], w=[dtok])
            else:
                ecopy(cast_eng(), dst[:, 0:kcn, c0:c0 + cw], st[:, 0:kcn, 0:cw], [("WST", b)], [dtok])
            yield

    def bc(ap, shape):
        return ap.to_broadcast(list(shape))

    def transpose_out(YZ, t, br):
        pb, ptok = psb()
        for kc in range(4):
            P.op("pe", lambda e, kc=kc, pb=pb: e.transpose(pb[:, kc * 128:(kc + 1) * 128], YZ[:, t, kc * 128:(kc + 1) * 128], IDB[:]),
                 r=[("YZ", t), "IDB"], w=[ptok])
        b = U() % 2
        yst = GV["YST"][b]
        ecopy("act", yst[:], pb[:, 0:512].rearrange("p (k c) -> p k c", k=4), [ptok], [("YST", b)])
        P.dma("sp", YTD[br, :, :, t * 128:(t + 1) * 128].rearrange("k p c -> p k c"), yst[:], r=[("YST", b)], w=[("YTD", br, t)])

    def z_phase(l, YZ, WST, cols, kind, post=None):
        for _ in z_phase_gen(l, YZ, WST, cols, kind, post):
            pass

    def z_phase_gen(l, YZ, WST, cols, kind, post=None):
        WZ = SB.alloc("wz", [128, KC, 512], BF16)
        col_list = [cols] if kind == "z" else list(cols)
        for ci, c in enumerate(col_list):
            load_w(W["w_in"][l, :, c:c + 512], KC, 512, WZ, "WZ", WST, True)
            for t in range(NT):
                ps, ptok = psf()
                for kc in range(KC):
                    P.op("pe", lambda e, kc=kc, ps=ps, t=t: e.matmul(ps[:, :], HT[:, kc, t * 128:(t + 1) * 128], WZ[:, kc, :], start=(kc == 0), stop=(kc == KC - 1)),
                         r=["HT", "WZ"], w=[ptok])
                if kind == "z":
                    P.op("act", lambda e, ps=ps, t=t: e.activation(YZ[:, t, :], ps[:, :], AF.Silu), r=[ptok], w=[("YZ", t)])
                    if post is not None:
                        P.op("pool", lambda e, t=t: e.tensor_tensor(YZ[:, t, :], YZ[:, t, :], post[:], ALU.mult), r=[("YZ", t), "NPOST"], w=[("YZ", t)])
                elif ci == 0:
                    P.op("act", lambda e, ps=ps, t=t: e.activation(YZ[:, t, :], ps[:, :], AF.Sigmoid), r=[ptok], w=[("YZ", t)])
                else:
                    b = t % 2
                    P.op("act", lambda e, ps=ps, b=b: e.activation(GV["ZT"][b][:], ps[:, :], AF.Silu), r=[ptok], w=[("ZT", b)])
                    P.op("dve", lambda e, t=t, b=b: e.tensor_tensor(YZ[:, t, :], YZ[:, t, :], GV["ZT"][b][:], ALU.mult), r=[("ZT", b), ("YZ", t)], w=[("YZ", t)])
                    if post is not None:
                        P.op("pool", lambda e, t=t: e.tensor_tensor(YZ[:, t, :], YZ[:, t, :], post[:], ALU.mult), r=[("YZ", t), "NPOST"], w=[("YZ", t)])
                yield

    class _Stop(Exception):
        pass
    import os
    astop = int(os.environ.get('ASTOP', '99'))

    def chk(k):
        if astop == k:
            raise _Stop()

    def layers():
      for l in range(nlayers):
          P.barrier()
          arena()
          def phase_0():
              P.dma("sp", GCOL[:], W["norm_g"][l].rearrange("(k p) -> p k", p=128), w=["GCOL"], allow_slow_non_contiguous=True)
              for i, (nm, n) in enumerate([("dt_bias", 4), ("a_log", 4), ("fox_f_bias", 8), ("mlstm_i_bias", 4), ("mlstm_f_bias", 4)]):
                  o0 = [0, 4, 8, 16, 20][i]
                  P.dma("sp", SMALLP[:, o0:o0 + n], W[nm][l].partition_broadcast(128), w=["SMALLP"])
              for j in range(4):
                  P.dma("sp", CONVA[:, :, j], W["conv_a"][l, j].rearrange("(c p) -> p c", p=128), w=["CONVA"], allow_slow_non_contiguous=True)
                  P.dma("sp", CONVC[:, :, j], W["conv_c"][l, j].rearrange("(c p) -> p c", p=128), w=["CONVC"], allow_slow_non_contiguous=True)
              P.dma("sp", NA[:], W["norm_a"][l].partition_broadcast(128), w=["NA", "NPOST"])
              P.dma("sp", NCW[:], W["norm_c"][l].partition_broadcast(128), w=["NCW", "NPOST"])
              P.op("act", lambda e: e.activation(NEGA[:], SMALLP[:, 4:8], AF.Exp), r=["SMALLP"], w=["NEGA"])
              P.op("dve", lambda e: e.tensor_scalar(NEGA[:], NEGA[:], -1.0, None, ALU.mult), r=["NEGA"], w=["NEGA"])
              XT = [SB.alloc("xt", [128, DM], F32) for _ in range(2)]
              XN = [SB.alloc("xn", [128, DM], BF16) for _ in range(2)]
              SQJ = SB.alloc("sqj", [128, DM], F32)
              SSQ = SB.alloc("ssq", [128, NT], F32)
              RSTD = SB.alloc("rstd", [128, NT], F32)
              src = x_in if l == 0 else xs
              P.op("pool", lambda e: e.memset(SSQ[:], 0.0), w=["SSQ"])
              for t in range(NT):
                  b = t % 2
                  P.dma("sp", XT[b][:], src[t * 128:(t + 1) * 128, :], r=[("XS", t)], w=[("XT", b)])
                  P.op("act", lambda e, b=b, t=t: e.activation(SQJ[:], XT[b][:], AF.Square, accum_out=SSQ[:, t:t + 1]), r=[("XT", b), "SSQ"], w=["SQJ", ("SSQ", t)])
                  P.op("act", lambda e, t=t: e.activation(RSTD[:, t:t + 1], SSQ[:, t:t + 1], AF.Sqrt, bias=EPS, scale=1.0 / DM), r=[("SSQ", t)], w=[("RSTD", t)])
                  P.op("dve", lambda e, t=t: e.reciprocal(RSTD[:, t:t + 1], RSTD[:, t:t + 1]), r=[("RSTD", t)], w=[("RSTD", t)])
                  P.op("dve", lambda e, b=b, t=t: e.tensor_scalar(XN[b][:], XT[b][:], RSTD[:, t:t + 1], None, ALU.mult), r=[("XT", b), ("RSTD", t)], w=[("XN", b)])
                  pb, ptok = psb()
                  for kc in range(KC):
                      P.op("pe", lambda e, kc=kc, pb=pb, b=b: e.transpose(pb[:, kc * 128:(kc + 1) * 128], XN[b][:, kc * 128:(kc + 1) * 128], IDB[:]),
                           r=[("XN", b), "IDB"], w=[ptok])
                  ecopy(("act", "dve")[t % 2], HT[:, :, t * 128:(t + 1) * 128], pb[:].rearrange("p (k c) -> p k c", k=KC), [ptok], ["HT"])
              dump("ht", HT[:, 0, :], "HT", [128, T], BF16)
          if "0" in phases:
              phase_0()

          def phase_G():
              P.barrier()
              arena()
              WST = [SB.alloc("wst", [128, KC, 256], F32) for _ in range(2)]
              WSM = SB.alloc("wsm", [128, KC, 24], BF16)
              GP = SB.alloc("gp", [128, NT, 24], F32)
              for i, (nm, n) in enumerate([("a_beta", 4), ("a_alpha", 4), ("b_f", 8), ("c_i", 4), ("c_f", 4)]):
                  o0 = [0, 4, 8, 16, 20][i]
                  P.dma("sp", WST[0][:, :, o0:o0 + n], W["w_in"][l, :, OFF[nm]:OFF[nm] + n].rearrange("(k p) c -> p k c", p=128), w=[("WST", 0)])
              for kc in range(KC):
                  P.op("pool", lambda e, kc=kc: e.tensor_scalar(WSM[:, kc, :], WST[0][:, kc, 0:24], GCOL[:, kc:kc + 1], None, ALU.mult), r=[("WST", 0), "GCOL"], w=["WSM"])
              for t0 in (0, 16):
                  ps, ptok = psf()
                  for t in range(t0, t0 + 16):
                      for kc in range(KC):
                          P.op("pe", lambda e, kc=kc, ps=ps, t=t, t0=t0: e.matmul(ps[:, (t - t0) * 24:(t - t0 + 1) * 24], HT[:, kc, t * 128:(t + 1) * 128], WSM[:, kc, :], start=(kc == 0), stop=(kc == KC - 1)),
                               r=["HT", "WSM"], w=[ptok])
                  P.op("dve", lambda e, ps=ps, t0=t0: e.tensor_copy(GP[:, t0:t0 + 16, :], ps[:, 0:384].rearrange("p (t c) -> p t c", c=24)), r=[ptok], w=["GP"])
              X16 = SB.alloc("x16", [128, NT, 16], F32)
              NX = SB.alloc("nx", [128, NT, 16], F32)
              SPL = SB.alloc("spl", [128, NT, 16], F32)
              IP = SB.alloc("ip", [128, NT, 4], F32)
              LF12 = SB.alloc("lf12", [128, NT, 12], F32)
              TOTA = SB.alloc("tota", [128, NT, 12], F32)
              SCA = [SB.alloc("sca", [128, NT, 12], F32) for _ in range(2)]
              TMP4 = SB.alloc("tmp4", [128, NT, 4], F32)
              MX4 = SB.alloc("mx4", [128, 4], F32)
              MST = SB.alloc("mst", [128, 4], F32)
              MXT = SB.alloc("mxt", [4, 4], F32)
              MXD = SB.alloc("mxd", [4, 4], F32)

              def spb(o0, n):
                  return bc(SMALLP[:, o0:o0 + n].unsqueeze(1), [128, NT, n])
              P.op("dve", lambda e: e.tensor_tensor(X16[:, :, 0:4], GP[:, :, 4:8], spb(0, 4), ALU.add), r=["GP", "SMALLP"], w=["X16"])
              P.op("dve", lambda e: e.scalar_tensor_tensor(X16[:, :, 4:12], GP[:, :, 8:16], -1.0, spb(8, 8), ALU.mult, ALU.subtract), r=["GP", "SMALLP"], w=["X16"])
              P.op("dve", lambda e: e.scalar_tensor_tensor(X16[:, :, 12:16], GP[:, :, 20:24], -1.0, spb(20, 4), ALU.mult, ALU.subtract), r=["GP", "SMALLP"], w=["X16"])
              P.op("act", lambda e: e.mul(NX[:], X16[:], -1.0), r=["X16"], w=["NX"])
              P.op("dve", lambda e: e.tensor_tensor(NX[:], NX[:], X16[:], ALU.min), r=["NX", "X16"], w=["NX"])
              P.op("act", lambda e: e.activation(NX[:], NX[:], AF.Exp), r=["NX"], w=["NX"])
              P.op("act", lambda e: e.activation(NX[:], NX[:], AF.Ln, bias=1.0, scale=1.0), r=["NX"], w=["NX"])
              P.op("dve", lambda e: e.tensor_scalar(SPL[:], X16[:], 0.0, None, ALU.max), r=["X16"], w=["SPL"])
              P.op("dve", lambda e: e.tensor_tensor(SPL[:], SPL[:], NX[:], ALU.add), r=["SPL", "NX"], w=["SPL"])
              P.op("dve", lambda e: e.tensor_tensor(GG[:], SPL[:, :, 0:4], bc(NEGA[:].unsqueeze(1), [128, NT, 4]), ALU.mult), r=["SPL", "NEGA"], w=["GG"])
              P.op("dve", lambda e: e.tensor_scalar(LF12[:], SPL[:, :, 4:16], -1.0, None, ALU.mult), r=["SPL"], w=["LF12"])
              P.op("act", lambda e: e.activation(BETA[:], GP[:, :, 0:4], AF.Sigmoid), r=["GP"], w=["BETA"])
              P.op("dve", lambda e: e.tensor_scalar(NEGB[:], BETA[:], -1.0, None, ALU.mult), r=["BETA"], w=["NEGB"])
              P.op("dve", lambda e: e.tensor_tensor(IP[:], GP[:, :, 16:20], spb(16, 4), ALU.add), r=["GP", "SMALLP"], w=["IP"])
              ps, ptok = psf()
              P.op("pe", lambda e, ps=ps: e.matmul(ps[:, 0:128], UI[:], GG[:].rearrange("p t c -> p (t c)"), start=True, stop=True), r=["UI", "GG"], w=[ptok])
              P.op("pe", lambda e, ps=ps: e.matmul(ps[:, 128:256], ONF[:], GG[:].rearrange("p t c -> p (t c)"), start=True, stop=True), r=["ONF", "GG"], w=[ptok])
              GCf = SB.alloc("gcf", [128, 128], F32)
              GEf = SB.alloc("gef", [128, 128], F32)
              P.op("dve", lambda e, ps=ps: e.tensor_copy(GCf[:], ps[:, 0:128]), r=[ptok], w=["GCf"])
              P.op("dve", lambda e, ps=ps: e.tensor_copy(GEf[:], ps[:, 128:256]), r=[ptok], w=["GEf"])
              fl = lambda a: a[:].rearrange("p t c -> p (t c)")
              P.op("act", lambda e: e.activation(fl(EG), GCf[:], AF.Exp), r=["GCf"], w=["EG"])
              P.op("act", lambda e: e.activation(fl(GL), GEf[:], AF.Exp), r=["GEf"], w=["GL"])
              P.op("dve", lambda e: e.tensor_tensor(GEf[:], GEf[:], GCf[:], ALU.subtract), r=["GEf", "GCf"], w=["GEf"])
              P.op("act", lambda e: e.activation(fl(ED), GEf[:], AF.Exp), r=["GEf"], w=["ED"])
              P.op("dve", lambda e: e.tensor_tensor(BEG[:], BETA[:], EG[:], ALU.mult), r=["BETA", "EG"], w=["BEG"])
              ps, ptok = psf()
              P.op("pe", lambda e, ps=ps: e.matmul(ps[:, 0:384], ONF[:], fl(LF12), start=True, stop=True), r=["ONF", "LF12"], w=[ptok])
              P.op("dve", lambda e, ps=ps: e.tensor_copy(fl(TOTA), ps[:, 0:384]), r=[ptok], w=["TOTA"])
              P.op("dve", lambda e: e.tensor_copy(SCA[0][:], TOTA[:]), r=["TOTA"], w=[("SCA", 0)])
              cur = 0
              for s in (1, 2, 4, 8, 16):
                  nxt = 1 - cur
                  P.op("dve", lambda e, s=s, cur=cur, nxt=nxt: e.tensor_tensor(SCA[nxt][:, s:, :], SCA[cur][:, s:, :], SCA[cur][:, 0:NT - s, :], ALU.add), r=[("SCA", cur)], w=[("SCA", nxt)])
                  P.op("pool", lambda e, s=s, cur=cur, nxt=nxt: e.tensor_copy(SCA[nxt][:, 0:s, :], SCA[cur][:, 0:s, :]), r=[("SCA", cur)], w=[("SCA", nxt)])
                  cur = nxt
              P.op("dve", lambda e, cur=cur: e.tensor_copy(INCL[:], SCA[cur][:]), r=[("SCA", cur)], w=["INCL"])
              P.op("dve", lambda e: e.tensor_tensor(CUM[:], INCL[:], TOTA[:], ALU.subtract), r=["INCL", "TOTA"], w=["CUM"])
              ps, ptok = psf()
              P.op("pe", lambda e, ps=ps: e.matmul(ps[:, 0:384], UI[:], fl(LF12), start=True, stop=True), r=["UI", "LF12"], w=[ptok])
              P.op("dve", lambda e, ps=ps: e.tensor_tensor(fl(CUM), fl(CUM), ps[:, 0:384], ALU.add), r=[ptok, "CUM"], w=["CUM"])
              P.op("dve", lambda e: e.tensor_reduce(MX4[:], IP[:].rearrange("p t c -> p c t"), AX.X, ALU.max), r=["IP"], w=["MX4"])
              ps, ptok = psf()
              P.op("pe", lambda e, ps=ps: e.transpose(ps[0:4, 0:128], MX4[:], IDF[:]), r=["MX4", "IDF"], w=[ptok])
              P.op("dve", lambda e, ps=ps: e.tensor_reduce(MXT[:, 0:1], ps[0:4, 0:128], AX.X, ALU.max), r=[ptok], w=["MXT"])
              P.op("dve", lambda e: e.tensor_scalar(MXD[:], IDF[0:4, 0:4], MXT[:, 0:1], None, ALU.mult), r=["MXT", "IDF"], w=["MXD"])
              ps, ptok = psf()
              P.op("pe", lambda e, ps=ps: e.matmul(ps[:, 0:4], ONF[0:4, :], MXD[:], start=True, stop=True), r=["ONF", "MXD"], w=[ptok])
              P.op("dve", lambda e, ps=ps: e.tensor_copy(MST[:], ps[:, 0:4]), r=[ptok], w=["MST"])
              mstb = lambda: bc(MST[:].unsqueeze(1), [128, NT, 4])
              P.op("dve", lambda e: e.tensor_tensor(TMP4[:], INCL[:, :, 8:12], CUM[:, :, 8:12], ALU.subtract), r=["INCL", "CUM"], w=["TMP4"])
              P.op("dve", lambda e: e.tensor_tensor(TMP4[:], TMP4[:], mstb(), ALU.subtract), r=["TMP4", "MST"], w=["TMP4"])
              P.op("act", lambda e: e.activation(THR[:], TMP4[:], AF.Exp), r=["TMP4"], w=["THR"])
              P.op("dve", lambda e: e.tensor_tensor(TMP4[:], TMP4[:], IP[:], ALU.add), r=["TMP4", "IP"], w=["TMP4"])
              P.op("act", lambda e: e.activation(WP8[:], TMP4[:], AF.Exp), r=["TMP4"], w=["WP8"])
              P.op("dve", lambda e: e.tensor_scalar(WP8[:], WP8[:], 0.125, None, ALU.mult), r=["WP8"], w=["WP8"])
              P.op("act", lambda e: e.activation(GTT[:], TOTA[:, :, 8:12], AF.Exp), r=["TOTA"], w=["GTT"])
              dump("beta", BETA[:], "BETA", [128, NT, 4])
              dump("gg", GG[:], "GG", [128, NT, 4])
              dump("cum", CUM[:], "CUM", [128, NT, 12])
              dump("incl", INCL[:], "INCL", [128, NT, 12])
              dump("lf12", LF12[:], "LF12", [128, NT, 12])
          if "G" in phases:
              phase_G()

          def phase_A():
              P.barrier()
              arena()
              YZ = SB.alloc("yz", [128, NT, 512], BF16)
              YST = GV["YST"] = [SB.alloc("yst", [128, 4, 128], BF16) for _ in range(2)]
              WA = SB.alloc("wa", [128, KC, 1536], BF16)
              CT = SB.alloc("ct", [128, 12, 256], BF16)
              RAW = [SB.alloc("raw", [128, 259], F32) for _ in range(2)]
              ACC = [SB.alloc("acc", [128, 256], F32) for _ in range(2)]
              HALO = SB.alloc("halo", [128, 12, 3], F32)
              S32 = SB.alloc("s32", [128, 4, 128], F32)
              SBF = SB.alloc("sbf", [128, 4, 128], BF16)
              a1 = SB.off
              WST = [SB.alloc("wst", [128, KC, 256], F32) for _ in range(2)]
              a0 = SB.off
              run_rr([z_phase_gen(l, YZ, WST, OFF["a_z"], "z", post=NA), load_w_gen(W["w_in"][l, :, 0:1536], KC, 1536, WA, "WA", WST, True)])
              SB.off = a0
              P.barrier()
              SB.off = a1

              def mkset(sid):
                  B = dict(sid=sid)
                  for nm, shp, dt in (("SS8", [128, 12], F32), ("RS8", [128, 12], F32), ("JUNK", [128, 128], F32),
                                      ("QKN", [128, 8, 128], BF16), ("VB", [128, 4, 128], BF16), ("KBG", [128, 4, 128], BF16),
                                      ("KD", [128, 4, 128], BF16), ("QKT", [128, 8, 128], BF16), ("UG", [128, 4, 128], F32),
                                      ("E1", [128, 4, 128], F32), ("E2", [128, 4, 128], F32), ("QKM", [128, 4, 128], BF16),
                                      ("MP0", [128, 4, 128], F32), ("MP1", [128, 4, 128], F32), ("MPT0", [128, 4, 128], F32),
                                      ("MPT1", [128, 4, 128], F32), ("XTB", [128, 4, 128], BF16), ("NWT", [128, 4, 128], BF16),
                                      ("VNB", [128, 4, 128], BF16)):
                      B[nm] = SB.alloc(nm.lower(), shp, dt)
                  return B
              SETS = [mkset(0), mkset(1)]
              P.op("pool", lambda e: e.memset(HALO[:], 0.0), w=["HALO"])
              P.op("pool", lambda e: e.memset(S32[:], 0.0), w=["S32"])
              P.op("pool", lambda e: e.memset(SBF[:], 0.0), w=["SBF"])
              f4 = lambda a: a[:].rearrange("p h c -> p (h c)")

              def hb(ap2):
                  return bc(ap2.unsqueeze(2), [128, 4, 128])

              def conv_block(blk):
                  for ch in range(12):
                      ps, ptok = psf()
                      for kc in range(KC):
                          P.op("pe", lambda e, kc=kc, ps=ps, ch=ch, blk=blk: e.matmul(ps[:, 0:256], WA[:, kc, ch * 128:(ch + 1) * 128], HT[:, kc, blk * 256:(blk + 1) * 256], start=(kc == 0), stop=(kc == KC - 1)),
                               r=["WA", "HT"], w=[ptok])
                      b = ch % 2
                      raw, acc = RAW[b], ACC[b]
                      P.op("pool", lambda e, raw=raw, ch=ch: e.tensor_copy(raw[:, 0:3], HALO[:, ch, :]), r=["HALO"], w=[("RAW", b)])
                      P.op("act", lambda e, raw=raw, ps=ps: e.copy(raw[:, 3:259], ps[:, 0:256]), r=[ptok], w=[("RAW", b)])
                      P.op("pool", lambda e, raw=raw, ch=ch: e.tensor_copy(HALO[:, ch, :], raw[:, 256:259]), r=[("RAW", b)], w=["HALO"])
                      P.op("dve", lambda e, raw=raw, acc=acc, ch=ch: e.tensor_scalar(acc[:], raw[:, 3:259], CONVA[:, ch, 3:4], None, ALU.mult), r=[("RAW", b), "CONVA"], w=[("ACC", b)])
                      for j in (2, 1, 0):
                          P.op("dve", lambda e, raw=raw, acc=acc, ch=ch, j=j: e.scalar_tensor_tensor(acc[:], raw[:, j:j + 256], CONVA[:, ch, j:j + 1], acc[:], ALU.mult, ALU.add),
                               r=[("RAW", b), ("ACC", b), "CONVA"], w=[("ACC", b)])
                      P.op("act", lambda e, acc=acc, ch=ch: e.activation(CT[:, ch, :], acc[:], AF.Silu), r=[("ACC", b)], w=[("CT", ch)])

              def gdn_tile(t, tt, B):
                  sid = B["sid"]
                  T_ = lambda n: (n, sid)
                  SS8, RS8, JUNK, QKN, VB, KBG, KD, QKT, UG, E1, E2, QKM, XTB, NWT, VNB = (B[k] for k in ("SS8", "RS8", "JUNK", "QKN", "VB", "KBG", "KD", "QKT", "UG", "E1", "E2", "QKM", "XTB", "NWT", "VNB"))
                  MP = [B["MP0"], B["MP1"]]
                  MPT = [B["MPT0"], B["MPT1"]]
                  XTT, T1, O32, MN = UG, E1, E2, MP[0]
                  cs = slice(tt * 128, (tt + 1) * 128)
                  pA, tA = psb()
                  for ch in range(8):
                      P.op("pe", lambda e, ch=ch: e.transpose(pA[:, ch * 128:(ch + 1) * 128], CT[:, ch, cs], IDB[:]), r=[("CT", ch), "IDB"], w=[tA])
                  P.op("pool", lambda e: e.memset(SS8[:], 0.0), w=[T_("SS8")])
                  for j in range(8):
                      P.op("act", lambda e, j=j: e.activation(JUNK[:], pA[:, j * 128:(j + 1) * 128], AF.Square, accum_out=SS8[:, j:j + 1]), r=[tA, T_("SS8")], w=[T_("JUNK"), T_("SS8")])
                  P.op("act", lambda e: e.activation(RS8[:, 0:4], SS8[:, 0:4], AF.Sqrt, bias=128.0 * EPS, scale=128.0), r=[T_("SS8")], w=[T_("RS8")])
                  P.op("act", lambda e: e.activation(RS8[:, 4:8], SS8[:, 4:8], AF.Sqrt, bias=EPS, scale=1.0), r=[T_("SS8")], w=[T_("RS8")])
                  P.op("dve", lambda e: e.reciprocal(RS8[:, 0:8], RS8[:, 0:8]), r=[T_("RS8")], w=[T_("RS8")])
                  P.op("dve", lambda e: e.tensor_tensor(QKN[:], pA[:].rearrange("p (h c) -> p h c", h=8), bc(RS8[:, 0:8].unsqueeze(2), [128, 8, 128]), ALU.mult), r=[tA, T_("RS8")], w=[T_("QKN")])
                  yield
                  pV, tV = psb()
                  for ch in range(4):
                      P.op("pe", lambda e, ch=ch: e.transpose(pV[:, ch * 128:(ch + 1) * 128], CT[:, 8 + ch, cs], IDB[:]), r=[("CT", 8 + ch), "IDB"], w=[tV])
                  P.op("dve", lambda e: e.tensor_tensor(VB[:], pV[:, 0:512].rearrange("p (h c) -> p h c", h=4), hb(BETA[:, t, :]), ALU.mult), r=[tV, "BETA"], w=[T_("VB")])
                  P.op("pool", lambda e: e.tensor_tensor(KBG[:], QKN[:, 4:8, :], hb(BEG[:, t, :]), ALU.mult), r=[T_("QKN"), "BEG"], w=[T_("KBG")])
                  P.op("pool", lambda e: e.tensor_tensor(KD[:], QKN[:, 4:8, :], hb(ED[:, t, :]), ALU.mult), r=[T_("QKN"), "ED"], w=[T_("KD")])
                  yield
                  pT, tT = psb()
                  for j in range(8):
                      P.op("pe", lambda e, j=j: e.transpose(pT[:, j * 128:(j + 1) * 128], QKN[:, j, :], IDB[:]), r=[T_("QKN"), "IDB"], w=[tT])
                  ecopy("act", QKT[:], pT[:].rearrange("p (h c) -> p h c", h=8), [tT], [T_("QKT")])
                  yield
                  for h in range(4):
                      P.op("act", lambda e, h=h: e.mul(UG[:, h, :], UI[:], GG[:, t, h:h + 1]), r=["UI", "GG"], w=[T_("UG")])
                  pD1, tD1 = psf()
                  pD2, tD2 = psf()
                  for h in range(4):
                      P.op("pe", lambda e, h=h: e.matmul(pD1[:, h * 128:(h + 1) * 128], UG[:, h, :], SL[:], start=True, stop=True), r=[T_("UG"), "SL"], w=[tD1])
                      P.op("pe", lambda e, h=h: e.matmul(pD2[:, h * 128:(h + 1) * 128], SL[:], UG[:, h, :], start=True, stop=True), r=[T_("UG"), "SL"], w=[tD2])
                  P.op("act", lambda e: e.activation(f4(E1), pD1[:, :], AF.Exp), r=[tD1], w=[T_("E1")])
                  P.op("act", lambda e: e.activation(f4(E2), pD2[:, :], AF.Exp), r=[tD2], w=[T_("E2")])
                  yield
                  P.op("dve", lambda e: e.tensor_tensor(E1[:], E1[:], bc(SL[:].unsqueeze(1), [128, 4, 128]), ALU.mult), r=[T_("E1"), "SL"], w=[T_("E1")])
                  P.op("pool", lambda e: e.tensor_tensor(E2[:], E2[:], bc(UI[:].unsqueeze(1), [128, 4, 128]), ALU.mult), r=[T_("E2"), "UI"], w=[T_("E2")])
                  pK, tK = psf()
                  pQ, tQ = psf()
                  for h in range(4):
                      P.op("pe", lambda e, h=h: e.matmul(pK[:, h * 128:(h + 1) * 128], QKT[:, 4 + h, :], QKT[:, 4 + h, :], start=True, stop=True), r=[T_("QKT")], w=[tK])
                      P.op("pe", lambda e, h=h: e.matmul(pQ[:, h * 128:(h + 1) * 128], QKT[:, 4 + h, :], QKT[:, h, :], start=True, stop=True), r=[T_("QKT")], w=[tQ])
                  P.op("dve", lambda e: e.tensor_tensor(f4(E1), pK[:, :], f4(E1), ALU.mult), r=[tK, T_("E1")], w=[T_("E1")])
                  for h in range(4):
                      P.op("act", lambda e, h=h: e.mul(MN[:, h, :], E1[:, h, :], NEGB[:, t, h:h + 1]), r=[T_("E1"), "NEGB"], w=[T_("MP0")])
                  P.op("dve", lambda e: e.tensor_tensor(f4(QKM), pQ[:, :], f4(E2), ALU.mult), r=[tQ, T_("E2")], w=[T_("QKM")])
                  yield
                  pM, tM = psf()
                  for h in range(4):
                      P.op("pe", lambda e, h=h: e.matmul(pM[:, h * 128:(h + 1) * 128], MN[:, h, :], IDF[:], start=True, stop=True), r=[T_("MP0"), "IDF"], w=[tM])
                  ecopy("act", f4(MPT[0]), pM[:, :], [tM], [T_("MPT0")])
                  P.op("dve", lambda e: e.tensor_tensor(XTT[:], MPT[0][:], bc(IDF[:].unsqueeze(1), [128, 4, 128]), ALU.add), r=[T_("MPT0"), "IDF"], w=[T_("UG")])
                  yield
                  cur = 0
                  for lev in range(1, 7):
                      nxt = 1 - cur
                      last = lev == 6
                      p1, t1 = psf()
                      for h in range(4):
                          P.op("pe", lambda e, h=h, p1=p1, cur=cur: e.matmul(p1[:, h * 128:(h + 1) * 128], MPT[cur][:, h, :], MP[cur][:, h, :], start=True, stop=True), r=[T_("MP%d" % cur), T_("MPT%d" % cur)], w=[t1])
                      if not last:
                          p2, t2 = psf()
                          for h in range(4):
                              P.op("pe", lambda e, h=h, p2=p2, cur=cur: e.matmul(p2[:, h * 128:(h + 1) * 128], MP[cur][:, h, :], MPT[cur][:, h, :], start=True, stop=True), r=[T_("MP%d" % cur), T_("MPT%d" % cur)], w=[t2])
                      ecopy("act", f4(MP[nxt]), p1[:, :], [t1], [T_("MP%d" % nxt)])
                      if not last:
                          ecopy("dve", f4(MPT[nxt]), p2[:, :], [t2], [T_("MPT%d" % nxt)])
                      yield
                      p3, t3 = psf()
                      for h in range(4):
                          P.op("pe", lambda e, h=h, p3=p3, nxt=nxt: e.matmul(p3[:, h * 128:(h + 1) * 128], MP[nxt][:, h, :], XTT[:, h, :], start=True, stop=True), r=[T_("MP%d" % nxt), T_("UG")], w=[t3])
                      P.op("dve", lambda e, p3=p3: e.tensor_tensor(f4(XTT), f4(XTT), p3[:, :], ALU.add), r=[t3, T_("UG")], w=[T_("UG")])
                      cur = nxt
                      yield
                  ecopy("act", XTB[:], XTT[:], [T_("UG")], [T_("XTB")])
                  pW, tW = psf()
                  for h in range(4):
                      P.op("pe", lambda e, h=h: e.matmul(pW[:, h * 128:(h + 1) * 128], KBG[:, h, :], XTB[:, h, :], start=True, stop=True), r=[T_("KBG"), T_("XTB")], w=[tW])
                  P.op("act", lambda e: e.mul(f4(NWT), pW[:, :], -1.0), r=[tW], w=[T_("NWT")])
                  yield
                  pN, tN = psf()
                  for h in range(4):
                      P.op("pe", lambda e, h=h: e.matmul(pN[:, h * 128:(h + 1) * 128], XTB[:, h, :], VB[:, h, :], start=True, stop=False), r=[T_("XTB"), T_("VB")], w=[tN])
                      P.op("pe", lambda e, h=h: e.matmul(pN[:, h * 128:(h + 1) * 128], NWT[:, h, :], SBF[:, h, :], start=False, stop=True), r=[T_("NWT"), "SBF"], w=[tN])
                  ecopy("dve", f4(VNB), pN[:, :], [tN], [T_("VNB")])
                  p1, t1 = psf()
                  p2, t2 = psf()
                  for h in range(4):
                      P.op("pe", lambda e, h=h: e.matmul(p1[:, h * 128:(h + 1) * 128], QKT[:, h, :], SBF[:, h, :], start=True, stop=True), r=[T_("QKT"), "SBF"], w=[t1])
                      P.op("pe", lambda e, h=h: e.matmul(p2[:, h * 128:(h + 1) * 128], QKM[:, h, :], VNB[:, h, :], start=True, stop=True), r=[T_("QKM"), T_("VNB")], w=[t2])
                  pS, tS = psf()
                  for h in range(4):
                      P.op("pe", lambda e, h=h: e.matmul(pS[:, h * 128:(h + 1) * 128], KD[:, h, :], VNB[:, h, :], start=True, stop=True), r=[T_("KD"), T_("VNB")], w=[tS])
                  for h in range(4):
                      P.op("dve", lambda e, h=h: e.scalar_tensor_tensor(S32[:, h, :], S32[:, h, :], GL[:, t, h:h + 1], pS[:, h * 128:(h + 1) * 128], ALU.mult, ALU.add), r=[tS, "S32", "GL"], w=["S32"])
                  ecopy("act", SBF[:], S32[:], ["S32"], ["SBF"])
                  P.op("dve", lambda e: e.tensor_tensor(T1[:], p1[:, :].rearrange("p (h c) -> p h c", h=4), hb(EG[:, t, :]), ALU.mult), r=[t1, "EG"], w=[T_("E1")])
                  P.op("dve", lambda e: e.tensor_tensor(f4(O32), f4(T1), p2[:, :], ALU.add), r=[t2, T_("E1")], w=[T_("E2")])
                  yield
                  for h in range(4):
                      P.op("act", lambda e, h=h: e.activation(JUNK[:], O32[:, h, :], AF.Square, accum_out=SS8[:, 8 + h:9 + h]), r=[T_("E2"), T_("SS8")], w=[T_("JUNK"), T_("SS8")])
                  P.op("act", lambda e: e.activation(RS8[:, 8:12], SS8[:, 8:12], AF.Sqrt, bias=EPS, scale=1.0 / 128), r=[T_("SS8")], w=[T_("RS8")])
                  P.op("dve", lambda e: e.reciprocal(RS8[:, 8:12], RS8[:, 8:12]), r=[T_("RS8")], w=[T_("RS8")])
                  P.op("dve", lambda e: e.tensor_tensor(O32[:], O32[:], hb(RS8[:, 8:12]), ALU.mult), r=[T_("E2"), T_("RS8")], w=[T_("E2")])
                  P.op("dve", lambda e: e.tensor_tensor(YZ[:, t, :], f4(O32), YZ[:, t, :], ALU.mult), r=[T_("E2"), ("YZ", t)], w=[("YZ", t)])
                  transpose_out(YZ, t, 0)

              for blk in range(16):
                  conv_block(blk)
                  run_rr([gdn_tile(blk * 2, 0, SETS[0]), gdn_tile(blk * 2 + 1, 1, SETS[1])])
          if "A" in phases:
              phase_A()

          def phase_B():
              P.barrier()
              arena()
              YZ = SB.alloc("yz", [128, NT, 512], BF16)
              WST = [SB.alloc("wst", [128, KC, 256], F32) for _ in range(2)]
              YST = GV["YST"] = [SB.alloc("yst", [128, 4, 128], BF16) for _ in range(2)]
              a0 = SB.off
              z_phase(l, YZ, WST, OFF["b_z"], "z")
              SB.off = a0
              P.barrier()
              WB = SB.alloc("wb", [128, KC, 384], BF16)
              QT = SB.alloc("qt", [128, T], BF16)
              KT = SB.alloc("kt", [128, T], BF16)
              VA = SB.alloc("va", [128, NT, 2, 65], BF16)
              BK = SB.alloc("bk", [128, NT, 16], F32)
              PT = [SB.alloc("pt", [128, 512], BF16) for _ in range(2)]
              RL = SB.alloc("rl", [128, 4], F32)
              OT = SB.alloc("ot", [128, 4, 64], F32)
              P.op("pool", lambda e: e.memset(VA[:], 1.0), w=["VA"])
              for pj in range(4):
                  for part in range(3):
                      c0 = OFF["b_qkv"] + part * 512 + pj * 128
                      load_w(W["w_in"][l, :, c0:c0 + 128], KC, 128, WB[:, :, part * 128:(part + 1) * 128], "WB", WST, True)
                  for blk in range(8):
                      for part, dst in ((0, QT), (1, KT)):
                          ps, ptok = psf()
                          for kc in range(KC):
                              P.op("pe", lambda e, kc=kc, ps=ps, part=part, blk=blk: e.matmul(ps[:, :], WB[:, kc, part * 128:(part + 1) * 128], HT[:, kc, blk * 512:(blk + 1) * 512], start=(kc == 0), stop=(kc == KC - 1)),
                                   r=["WB", "HT"], w=[ptok])
                          if part == 0:
                              P.op("act", lambda e, ps=ps, blk=blk: e.mul(QT[:, blk * 512:(blk + 1) * 512], ps[:, :], 0.125), r=[ptok], w=["QT"])
                          else:
                              P.op("dve", lambda e, ps=ps, blk=blk: e.tensor_copy(KT[:, blk * 512:(blk + 1) * 512], ps[:, :]), r=[ptok], w=["KT"])
                  for t in range(NT):
                      ps, ptok = psf()
                      for kc in range(KC):
                          P.op("pe", lambda e, kc=kc, ps=ps, t=t: e.matmul(ps[:, 0:128], HT[:, kc, t * 128:(t + 1) * 128], WB[:, kc, 256:384], start=(kc == 0), stop=(kc == KC - 1)),
                               r=["WB", "HT"], w=[ptok])
                      P.op("dve", lambda e, ps=ps, t=t: e.tensor_copy(VA[:, t, :, 0:64], ps[:, 0:128].rearrange("p (h c) -> p h c", h=2)), r=[ptok], w=["VA"])
                  for hh in range(2):
                      h = 2 * pj + hh
                      hp = hh * 64
                      P.op("dve", lambda e, h=h: e.tensor_tensor(BK[:], bc(INCL[:, 1:NT:2, h:h + 1].rearrange("p t c -> p c t"), [128, NT, 16]),
                                                                  bc(CUM[:, :, h:h + 1], [128, NT, 16]), ALU.subtract), r=["INCL", "CUM"], w=["BK"])
                      for qb in range(8):
                          po, otok = PSF[4 + qb % 2], ("psf", 4 + qb % 2)
                          nk = 4 * qb + 4

                          def emitS(kt, qb=qb):
                              c0 = max(kt - 4 * qb, 0) * 128
                              sci = kt % 4
                              ps, stok = PSF[sci], ("psf", sci)
                              P.op("pe", lambda e, ps=ps, kt=kt, qb=qb, c0=c0, hp=hp: e.matmul(ps[:, c0:512], KT[hp:hp + 64, kt * 128:(kt + 1) * 128], QT[hp:hp + 64, qb * 512 + c0:(qb + 1) * 512], start=True, stop=True),
                                   r=["KT", "QT"], w=[stok])
                          LA = 2
                          for kt in range(min(LA, nk)):
                              emitS(kt)
                          for kt in range(nk):
                              if kt + LA < nk:
                                  emitS(kt + LA)
                              i = max(kt - 4 * qb, 0)
                              sci = kt % 4
                              ps, stok = PSF[sci], ("psf", sci)
                              pb_ = kt % 2
                              pt = PT[pb_]
                              for g2 in range(2):
                                  ca = max(g2 * 256, i * 128)
                                  cb = (g2 + 1) * 256
                                  if ca >= cb:
                                      continue
                                  gi = 2 * qb + g2
                                  P.op("act", lambda e, ps=ps, pt=pt, ca=ca, cb=cb, kt=kt, gi=gi: e.activation(pt[:, ca:cb], ps[:, ca:cb], AF.Exp, bias=BK[:, kt, gi:gi + 1], scale=1.0),
                                       r=[stok, "BK"], w=[("PT", pb_, jq) for jq in range(ca // 128, cb // 128)])
                              for jq in range(i, 4):
                                  qt = 4 * qb + jq
                                  if kt == qt:
                                      P.op("pool", lambda e, pt=pt, jq=jq: e.tensor_tensor(pt[:, jq * 128:(jq + 1) * 128], pt[:, jq * 128:(jq + 1) * 128], UIB[:], ALU.mult),
                                           r=[("PT", pb_, jq), "UIB"], w=[("PT", pb_, jq)])
                                  P.op("pe", lambda e, po=po, pt=pt, jq=jq, kt=kt, qt=qt, hh=hh: e.matmul(po[:, jq * 65:(jq + 1) * 65], pt[:, jq * 128:(jq + 1) * 128], VA[:, kt, hh, :], start=(kt == 0 and jq == 0), stop=(kt == qt), skip_group_check=True),
                                       r=[("PT", pb_, jq), "VA"], w=[otok])
                          pov = po[:, 0:260].rearrange("p (j c) -> p j c", c=65)
                          P.op("dve", lambda e, pov=pov: e.reciprocal(RL[:], pov[:, :, 64]), r=[otok], w=["RL"])
                          P.op("dve", lambda e, pov=pov: e.tensor_tensor(OT[:], pov[:, :, 0:64], bc(RL[:].unsqueeze(2), [128, 4, 64]), ALU.mult), r=[otok, "RL"], w=["OT"])
                          for jq in range(4):
                              qt = 4 * qb + jq
                              P.op("pool", lambda e, jq=jq, qt=qt, h=h: e.tensor_tensor(YZ[:, qt, h * 64:(h + 1) * 64], OT[:, jq, :], YZ[:, qt, h * 64:(h + 1) * 64], ALU.mult),
                                   r=["OT", ("YZ", qt)], w=[("YZ", qt)])
              for t in range(NT):
                  transpose_out(YZ, t, 1)
          if "B" in phases:
              phase_B()
          def phase_C():
              P.barrier()
              arena()
              YZ = SB.alloc("yz", [128, NT, 512], BF16)
              WST = [SB.alloc("wst", [128, KC, 256], F32) for _ in range(2)]
              YST = GV["YST"] = [SB.alloc("yst", [128, 4, 128], BF16) for _ in range(2)]
              ZT = GV["ZT"] = [SB.alloc("zt", [128, 512], BF16) for _ in range(2)]
              WC = SB.alloc("wc", [128, KC, 1024], BF16)
              a0 = SB.off
              run_rr([z_phase_gen(l, YZ, WST, (OFF["c_o"], OFF["c_z"]), "oz", post=NCW),
                      load_w_gen(W["w_in"][l, :, OFF["c_qk"]:OFF["c_qk"] + 1024], KC, 1024, WC, "WC", WST, True)])
              SB.off = a0
              P.barrier()
              chk(20)
              CTC = SB.alloc("ctc", [128, 4, 512], BF16)
              RAW = [SB.alloc("raw", [128, 515], F32) for _ in range(2)]
              ACC = [SB.alloc("acc", [128, 512], F32) for _ in range(2)]
              HALOC = SB.alloc("halo", [128, 4, 3], F32)
              VAC = SB.alloc("vac", [128, 4, 129], BF16)
              KW = SB.alloc("kw", [128, 4, 64], BF16)
              KWT = SB.alloc("kwt", [128, 2, 128], BF16)
              AM = SB.alloc("am", [128, 4, 128], BF16)
              CS32 = SB.alloc("cs32", [128, 2, 129], F32)
              CG32 = SB.alloc("cg32", [128, 2, 129], F32)
              CGB = SB.alloc("cgb", [128, 2, 129], BF16)
              DEN = SB.alloc("den", [128, 4], F32)
              HH = SB.alloc("hh", [128, 4, 128], F32)
              JUNKC = SB.alloc("junk", [128, 128], F32)
              SS4 = SB.alloc("ss4", [128, 4], F32)
              P.op("pool", lambda e: e.memset(HALOC[:], 0.0), w=["HALOC"])
              P.op("pool", lambda e: e.memset(CS32[:], 0.0), w=["CS32"])
              P.op("pool", lambda e: e.memset(VAC[:], 1.0), w=["VAC"])
              f4 = lambda a: a[:].rearrange("p h c -> p (h c)")
              for blk in range(8):
                  for ch in range(4):
                      ps, ptok = psf()
                      for kc in range(KC):
                          P.op("pe", lambda e, kc=kc, ps=ps, ch=ch, blk=blk: e.matmul(ps[:, :], WC[:, kc, ch * 128:(ch + 1) * 128], HT[:, kc, blk * 512:(blk + 1) * 512], start=(kc == 0), stop=(kc == KC - 1)),
                               r=["WC", "HT"], w=[ptok])
                      b = U() % 2
                      raw, acc = RAW[b], ACC[b]
                      P.op("pool", lambda e, raw=raw, ch=ch: e.tensor_copy(raw[:, 0:3], HALOC[:, ch, :]), r=["HALOC"], w=[("RAW", b)])
                      P.op("act", lambda e, raw=raw, ps=ps: e.copy(raw[:, 3:515], ps[:, :]), r=[ptok], w=[("RAW", b)])
                      P.op("pool", lambda e, raw=raw, ch=ch: e.tensor_copy(HALOC[:, ch, :], raw[:, 512:515]), r=[("RAW", b)], w=["HALOC"])
                      P.op("dve", lambda e, raw=raw, acc=acc, ch=ch: e.tensor_scalar(acc[:], raw[:, 3:515], CONVC[:, ch, 3:4], None, ALU.mult), r=[("RAW", b), "CONVC"], w=[("ACC", b)])
                      for j in (2, 1, 0):
                          P.op("dve", lambda e, raw=raw, acc=acc, ch=ch, j=j: e.scalar_tensor_tensor(acc[:], raw[:, j:j + 512], CONVC[:, ch, j:j + 1], acc[:], ALU.mult, ALU.add),
                               r=[("RAW", b), ("ACC", b), "CONVC"], w=[("ACC", b)])
                      P.op("act", lambda e, acc=acc, ch=ch: e.activation(CTC[:, ch, :], acc[:], AF.Silu), r=[("ACC", b)], w=[("CTC", ch)])
                  for tt in range(4):
                      t = blk * 4 + tt
                      cs = slice(tt * 128, (tt + 1) * 128)
                      chk(21)
                      ps, ptok = psf()
                      for kc in range(KC):
                          P.op("pe", lambda e, kc=kc, ps=ps, t=t: e.matmul(ps[:, :], HT[:, kc, t * 128:(t + 1) * 128], WC[:, kc, 512:1024], start=(kc == 0), stop=(kc == KC - 1)),
                               r=["WC", "HT"], w=[ptok])
                      P.op("act", lambda e, ps=ps: e.copy(VAC[:, :, 0:128], ps[:, :].rearrange("p (h c) -> p h c", h=4)), r=[ptok], w=["VAC"])
                      pk, tk = psb()
                      for p_ in range(2):
                          P.op("pe", lambda e, p_=p_, pk=pk, cs=cs: e.transpose(pk[:, p_ * 128:(p_ + 1) * 128], CTC[:, 2 + p_, cs], IDB[:]), r=[("CTC", 2 + p_), "IDB"], w=[tk])
                      P.op("dve", lambda e, pk=pk, t=t: e.tensor_tensor(KW[:], pk[:, 0:256].rearrange("p (h c) -> p h c", h=4), bc(WP8[:, t, :].unsqueeze(2), [128, 4, 64]), ALU.mult), r=[tk, "WP8"], w=["KW"])
                      pt_, tt_ = psb()
                      for p_ in range(2):
                          P.op("pe", lambda e, p_=p_, pt_=pt_: e.transpose(pt_[:, p_ * 128:(p_ + 1) * 128], KW[:, 2 * p_:2 * p_ + 2, :].rearrange("p h c -> p (h c)"), IDB[:]), r=["KW", "IDB"], w=[tt_])
                      ecopy("act", KWT[:], pt_[:, 0:256].rearrange("p (h c) -> p h c", h=2), [tt_], ["KWT"])
                      chk(22)
                      paL = [psf(), psf()]
                      for h in range(4):
                          p_, hp = h // 2, (h % 2) * 64
                          pa, ta = paL[h % 2]
                          P.op("pe", lambda e, h=h, p_=p_, hp=hp, pa=pa, cs=cs: e.matmul(pa[:, p_ * 128:(p_ + 1) * 128], KWT[hp:hp + 64, p_, :], CTC[hp:hp + 64, p_, cs], start=True, stop=True),
                               r=["KWT", ("CTC", p_)], w=[ta])
                      for h in range(4):
                          pa, ta = paL[h % 2]
                          P.op("dve", lambda e, pa=pa, h=h: e.tensor_tensor(AM[:, h, :], pa[:, (h // 2) * 128:(h // 2 + 1) * 128], UI[:], ALU.mult), r=[ta, "UI"], w=["AM"])
                      chk(23)
                      for h in range(4):
                          p_, hp = h // 2, (h % 2) * 64
                          P.op("pool", lambda e, h=h, p_=p_, hp=hp, t=t: e.tensor_scalar(CG32[hp:hp + 64, p_, :], CS32[hp:hp + 64, p_, :], GTT[hp:hp + 64, t, h:h + 1], None, ALU.mult),
                               r=["CS32", "GTT"], w=["CG32"])
                      ecopy("act", CGB[:], CG32[:], ["CG32"], ["CGB"])
                      chk(24)
                      pn = []
                      for g in range(2):
                          pn.append(psf())
                      for h in range(4):
                          p_, hp = h // 2, (h % 2) * 64
                          png, tng = pn[h % 2]
                          o0 = (h // 2) * 129
                          P.op("pe", lambda e, h=h, png=png, o0=o0: e.matmul(png[:, o0:o0 + 129], AM[:, h, :], VAC[:, h, :], start=True, stop=False), r=["AM", "VAC"], w=[tng])
                          P.op("pe", lambda e, h=h, p_=p_, hp=hp, png=png, o0=o0, cs=cs: e.matmul(png[:, o0:o0 + 129], CTC[hp:hp + 64, p_, cs], CGB[hp:hp + 64, p_, :], start=False, stop=True), r=[("CTC", p_), "CGB"], w=[tng])
                      chk(25)
                      psu = []
                      for g in range(2):
                          psu.append(psf())
                      for h in range(4):
                          p_, hp = h // 2, (h % 2) * 64
                          pg, tg = psu[h // 2]
                          o0 = (h % 2) * 129
                          P.op("pe", lambda e, h=h, p_=p_, pg=pg, o0=o0: e.matmul(pg[:, o0:o0 + 129], KW[:, 2 * p_:2 * p_ + 2, :].rearrange("p h c -> p (h c)"), VAC[:, h, :], start=True, stop=True), r=["KW", "VAC"], w=[tg])
                      for h in range(4):
                          p_, hp = h // 2, (h % 2) * 64
                          pg, tg = psu[h // 2]
                          o0 = (h % 2) * 129
                          P.op("dve", lambda e, h=h, p_=p_, hp=hp, pg=pg, o0=o0: e.tensor_tensor(CS32[hp:hp + 64, p_, :], CG32[hp:hp + 64, p_, :], pg[hp:hp + 64, o0:o0 + 129], ALU.add), r=[tg, "CG32"], w=["CS32"])
                      chk(26)
                      for g in range(2):
                          png, tng = pn[g]
                          pv = png[:, 0:258].rearrange("p (h c) -> p h c", h=2)
                          P.op("dve", lambda e, pv=pv, g=g: e.tensor_copy(DEN[:, g:g + 3:2], pv[:, :, 128]), r=[tng], w=["DEN"])
                      P.op("dve", lambda e: e.scalar_tensor_tensor(DEN[:], DEN[:], -1.0, DEN[:], ALU.mult, ALU.max), r=["DEN"], w=["DEN"])
                      P.op("dve", lambda e, t=t: e.tensor_tensor(DEN[:], DEN[:], THR[:, t, :], ALU.max), r=["DEN", "THR"], w=["DEN"])
                      P.op("dve", lambda e: e.reciprocal(DEN[:], DEN[:]), r=["DEN"], w=["DEN"])
                      for g in range(2):
                          png, tng = pn[g]
                          pv = png[:, 0:258].rearrange("p (h c) -> p h c", h=2)
                          P.op("dve", lambda e, pv=pv, g=g: e.tensor_tensor(HH[:, g:g + 3:2, :], pv[:, :, 0:128], bc(DEN[:, g:g + 3:2].unsqueeze(2), [128, 2, 128]), ALU.mult), r=[tng, "DEN"], w=["HH"])
                      chk(27)
                      P.op("pool", lambda e: e.memset(SS4[:], 0.0), w=["SS4"])
                      for h in range(4):
                          P.op("act", lambda e, h=h: e.activation(JUNKC[:], HH[:, h, :], AF.Square, accum_out=SS4[:, h:h + 1]), r=["HH", "SS4"], w=["JUNKC", "SS4"])
                      P.op("act", lambda e: e.activation(SS4[:], SS4[:], AF.Sqrt, bias=EPS, scale=1.0 / 128), r=["SS4"], w=["SS4"])
                      P.op("dve", lambda e: e.reciprocal(SS4[:], SS4[:]), r=["SS4"], w=["SS4"])
                      P.op("dve", lambda e: e.tensor_tensor(HH[:], HH[:], bc(SS4[:].unsqueeze(2), [128, 4, 128]), ALU.mult), r=["HH", "SS4"], w=["HH"])
                      P.op("dve", lambda e, t=t: e.tensor_tensor(YZ[:, t, :], f4(HH), YZ[:, t, :], ALU.mult), r=["HH", ("YZ", t)], w=[("YZ", t)])
                      transpose_out(YZ, t, 2)
          if "C" in phases:
              phase_C()
          def phase_O():
              P.barrier()
              arena()
              WG = SB.alloc("wg", [128, KC, 3072], BF16)
              PJ = SB.alloc("pj", [128, 12, 1024], BF16)
              a1 = SB.off
              WST = [SB.alloc("wst", [128, KC, 256], F32) for _ in range(4)]
              for br in range(3):
                  c0 = OFF["gate"] + br * 1024
                  load_w(W["w_in"][l, :, c0:c0 + 1024], KC, 1024, WG[:, :, br * 1024:(br + 1) * 1024], "WG", WST, True)
                  pw = W[("proj_a", "proj_b", "proj_c")[br]]
                  load_w(pw[l], 4, 1024, PJ[:, br * 4:(br + 1) * 4, :], "PJ", WST, False)
              P.barrier()
              SB.off = a1
              YTB = [SB.alloc("ytb", [128, 12, 512], BF16) for _ in range(2)]
              SG = [SB.alloc("sg", [128, 512], F32) for _ in range(2)]
              MA = [SB.alloc("ma", [128, 512], F32) for _ in range(2)]
              MB = [SB.alloc("mb", [128, 8, 512], BF16) for _ in range(2)]
              for blk in range(8):
                  yb = blk % 2
                  for br in range(3):
                      P.dma("sp", YTB[yb][:, br * 4:(br + 1) * 4, :], YTD[br, :, :, blk * 512:(blk + 1) * 512].rearrange("k p c -> p k c"),
                            r=[("YTD", br, t) for t in range(blk * 4, blk * 4 + 4)], w=[("YTB", yb, br)])
                  mbb = blk % 2
                  for fc in range(8):
                      mab = fc % 2
                      for br in range(3):
                          pp, tp = psf()
                          for kc in range(4):
                              P.op("pe", lambda e, kc=kc, pp=pp, br=br, fc=fc, yb=yb: e.matmul(pp[:, :], PJ[:, br * 4 + kc, fc * 128:(fc + 1) * 128], YTB[yb][:, br * 4 + kc, :], start=(kc == 0), stop=(kc == 3)),
                                   r=["PJ", ("YTB", yb, br)], w=[tp])
                          pg, tg = psf()
                          for kc in range(KC):
                              P.op("pe", lambda e, kc=kc, pg=pg, br=br, fc=fc, blk=blk: e.matmul(pg[:, :], WG[:, kc, br * 1024 + fc * 128:br * 1024 + (fc + 1) * 128], HT[:, kc, blk * 512:(blk + 1) * 512], start=(kc == 0), stop=(kc == KC - 1)),
                                   r=["WG", "HT"], w=[tg])
                          sb_ = br % 2
                          P.op("act", lambda e, pg=pg, sb_=sb_: e.activation(SG[sb_][:], pg[:, :], AF.Sigmoid), r=[tg], w=[("SG", sb_)])
                          if br == 0:
                              P.op("dve", lambda e, pp=pp, sb_=sb_, mab=mab: e.tensor_tensor(MA[mab][:], SG[sb_][:], pp[:, :], ALU.mult), r=[tp, ("SG", sb_)], w=[("MA", mab)])
                          else:
                              P.op("dve", lambda e, pp=pp, sb_=sb_: e.tensor_tensor(SG[sb_][:], SG[sb_][:], pp[:, :], ALU.mult), r=[tp, ("SG", sb_)], w=[("SG", sb_)])
                              if br == 1:
                                  P.op("pool", lambda e, sb_=sb_, mab=mab: e.tensor_tensor(MA[mab][:], MA[mab][:], SG[sb_][:], ALU.add), r=[("MA", mab), ("SG", sb_)], w=[("MA", mab)])
                              else:
                                  P.op("pool", lambda e, sb_=sb_, mab=mab, mbb=mbb, fc=fc: e.tensor_tensor(MB[mbb][:, fc, :], MA[mab][:], SG[sb_][:], ALU.add), r=[("MA", mab), ("SG", sb_)], w=[("MB", mbb)])
                  P.dma("sp", MTD[:, :, blk * 512:(blk + 1) * 512].rearrange("k p c -> p k c"), MB[mbb][:], r=[("MB", mbb)], w=[("MTD", blk)])
              P.barrier()
              arena()
              WST = [SB.alloc("wst", [128, KC, 256], F32) for _ in range(2)]
              WO = SB.alloc("wo", [128, KC, 1024], BF16)
              load_w(W["w_out"][l], KC, 1024, WO, "WO", WST, False)
              MT = [SB.alloc("mt", [128, KC, 128], BF16) for _ in range(2)]
              XR = [SB.alloc("xr", [128, DM], F32) for _ in range(2)]
              XO = [SB.alloc("xo", [128, DM], F32) for _ in range(2)]
              last = (l == nlayers - 1) and final_norm
              if last:
                  FG = SB.alloc("fg", [128, DM], F32)
                  SQJ2 = SB.alloc("sqj", [128, DM], F32)
                  FS = SB.alloc("fs", [128, NT], F32)
                  P.dma("sp", FG[:], W["final_g"].partition_broadcast(128), w=["FG"])
                  P.op("pool", lambda e: e.memset(FS[:], 0.0), w=["FS"])
              src = x_in if l == 0 else xs
              for t in range(NT):
                  b = t % 2
                  P.dma("sp", MT[b][:], MTD[:, :, t * 128:(t + 1) * 128].rearrange("k p c -> p k c"), r=[("MTD", t // 4)], w=[("MT", b)])
                  P.dma("sp", XR[b][:], src[t * 128:(t + 1) * 128, :], r=[("XS", t)], w=[("XR", b)])
                  for half in range(2):
                      ps, ptok = psf()
                      for kc in range(KC):
                          P.op("pe", lambda e, kc=kc, ps=ps, b=b, half=half: e.matmul(ps[:, :], MT[b][:, kc, :], WO[:, kc, half * 512:(half + 1) * 512], start=(kc == 0), stop=(kc == KC - 1)),
                               r=[("MT", b), "WO"], w=[ptok])
                      P.op("dve", lambda e, ps=ps, b=b, half=half: e.tensor_tensor(XO[b][:, half * 512:(half + 1) * 512], XR[b][:, half * 512:(half + 1) * 512], ps[:, :], ALU.add), r=[ptok, ("XR", b)], w=[("XO", b)])
                  if not last:
                      P.dma("sp", xs[t * 128:(t + 1) * 128, :], XO[b][:], r=[("XO", b)], w=[("XS", t)])
                      if "x1" in dbg:
                          P.dma("sp", out[t * 128:(t + 1) * 128, :], XO[b][:], r=[("XO", b)])
                  else:
                      P.op("act", lambda e, b=b, t=t: e.activation(SQJ2[:], XO[b][:], AF.Square, accum_out=FS[:, t:t + 1]), r=[("XO", b), "FS"], w=["SQJ2", ("FS", t)])
                      P.op("act", lambda e, t=t: e.activation(FS[:, t:t + 1], FS[:, t:t + 1], AF.Sqrt, bias=EPS, scale=1.0 / DM), r=[("FS", t)], w=[("FS", t)])
                      P.op("dve", lambda e, t=t: e.reciprocal(FS[:, t:t + 1], FS[:, t:t + 1]), r=[("FS", t)], w=[("FS", t)])
                      P.op("dve", lambda e, b=b, t=t: e.scalar_tensor_tensor(XO[b][:], XO[b][:], FS[:, t:t + 1], FG[:], ALU.mult, ALU.mult), r=[("XO", b), ("FS", t), "FG"], w=[("XO", b)])
                      P.dma("sp", out[t * 128:(t + 1) * 128, :], XO[b][:], r=[("XO", b)], w=[("OUT", t)])
          if "O" in phases:
              phase_O()

    try:
        layers()
    except _Stop:
        pass
    P.emit()
    return nc, dbg_out


def kernel(**inputs):
    nc, _ = build()
    x = np.ascontiguousarray(inputs["x"], dtype=np.float32)
    wmap = {n: np.ascontiguousarray(inputs[n], dtype=np.float32) for n in WNAMES}
    in_maps = []
    for c in range(8):
        m = dict(wmap)
        m["x"] = x[c]
        in_maps.append(m)
    res = run_bass_kernel_spmd(nc, in_maps, core_ids=list(range(8)))
    return np.stack([np.asarray(r["out"]) for r in res.results], axis=0).astype(np.float32)
```
